# Optimizing a Trainium2 kernel written in Bass

```python
import math
import jax, jax.numpy as jnp
from jax import lax
import numpy as np

D_MODEL = 1024
BATCH = 1
SEQ = 16384
DEPTH = 2
DEC_BATCH = 8
DEC_SEQ = 4096
PAST_LEN = 128

CONV_W = D_MODEL
CONV_K = 3
RET_HEADS = 4
RET_DK = 256
RET_DV = 512
RET_QK = RET_HEADS * RET_DK
RET_V = RET_HEADS * RET_DV
CHUNK = 128
D_FF = 4 * D_MODEL
N_BRANCH = 2
IN_COLS = 3 * CONV_W + 2 * RET_QK + 2 * RET_V + N_BRANCH * D_MODEL
NORM_EPS = 1e-6
GN_EPS = 1e-5
ROPE_BASE = 10000.0

kernel_name = "hybrid_conv_retention_encoder"


def _rmsnorm(x, g):
    xf = x.astype(jnp.float32)
    xf = xf * lax.rsqrt(jnp.mean(xf * xf, axis=-1, keepdims=True) + NORM_EPS)
    return xf.astype(x.dtype) * g


def _rotary(x, cos, sin):
    x1, x2 = jnp.split(x, 2, axis=-1)
    return jnp.concatenate([x1 * cos - x2 * sin, x1 * sin + x2 * cos], axis=-1)


def _retention_dir(q, k, v, log_gamma, include_diag):
    b, h, s, dk = q.shape
    dv = v.shape[-1]
    n = s // CHUNK
    dt = q.dtype
    qc = q.reshape(b, h, n, CHUNK, dk)
    kc = k.reshape(b, h, n, CHUNK, dk)
    vc = v.reshape(b, h, n, CHUNK, dv)
    idx = jnp.arange(CHUNK, dtype=jnp.float32)
    diff = idx[:, None] - idx[None, :]
    mask = diff >= 0 if include_diag else diff > 0
    lg = log_gamma[:, None, None]
    decay_in = (jnp.exp(jnp.where(mask, diff, 0.0)[None] * lg) * mask[None]).astype(dt)
    scores = jnp.einsum('bhncd,bhnjd->bhncj', qc, kc) * decay_in[None, :, None]
    inner = jnp.einsum('bhncj,bhnje->bhnce', scores, vc)
    q_dec = jnp.exp((idx[None, :] + 1.0) * log_gamma[:, None]).astype(dt)
    k_dec = jnp.exp((CHUNK - 1.0 - idx[None, :]) * log_gamma[:, None]).astype(dt)
    c_dec = jnp.exp(CHUNK * log_gamma).astype(dt)
    qs = jnp.moveaxis(qc * q_dec[None, :, None, :, None], 2, 0)
    ks = jnp.moveaxis(kc * k_dec[None, :, None, :, None], 2, 0)
    vs = jnp.moveaxis(vc, 2, 0)

    def step(state, xs):
        qi, ki, vi = xs
        out = jnp.einsum('bhcd,bhde->bhce', qi, state)
        state = state * c_dec[None, :, None, None] + jnp.einsum('bhcd,bhce->bhde', ki, vi)
        return state, out

    s0 = jnp.zeros((b, h, dk, dv), dtype=v.dtype)
    _, cross = lax.scan(step, s0, (qs, ks, vs))
    cross = jnp.moveaxis(cross, 0, 2)
    return (inner + cross).reshape(b, h, s, dv)


def _layer(x, norm_mix, w_in, conv_w, w_conv_out, ret_decay_fwd, ret_decay_bwd,
           ret_gn_w, ret_gn_b, w_ret_out, gate_b, w_mix_out, norm_mlp, w_mlp_in, w_mlp_out):
    b, s, _ = x.shape
    h = _rmsnorm(x, norm_mix)
    proj = h @ w_in
    splits = np.cumsum([CONV_W, CONV_W, CONV_W, RET_QK, RET_QK, RET_V, RET_V]).tolist()
    cb, cc, cx, q, k, v, g_ret, gates = jnp.split(proj, splits, axis=-1)

    z = cc * cx
    zp = jnp.pad(z, ((0, 0), (1, 1), (0, 0)))
    zc = conv_w[0] * zp[:, :-2] + conv_w[1] * zp[:, 1:-1] + conv_w[2] * zp[:, 2:]
    y_conv = (cb * zc) @ w_conv_out

    pos = jnp.arange(s, dtype=jnp.float32)
    theta = 1.0 / (ROPE_BASE ** jnp.linspace(0.0, 1.0, RET_DK // 2, dtype=jnp.float32))
    ang = pos[:, None] * theta[None, :]
    cos = jnp.cos(ang)[:, None, :].astype(x.dtype)
    sin = jnp.sin(ang)[:, None, :].astype(x.dtype)
    q = _rotary(q.reshape(b, s, RET_HEADS, RET_DK), cos, sin)
    k = _rotary(k.reshape(b, s, RET_HEADS, RET_DK), cos, sin) * (RET_DK ** -0.5)
    v = v.reshape(b, s, RET_HEADS, RET_DV)
    q, k, v = (jnp.transpose(t, (0, 2, 1, 3)) for t in (q, k, v))
    lg_f = jnp.log1p(-jnp.exp(ret_decay_fwd.astype(jnp.float32)))
    lg_b = jnp.log1p(-jnp.exp(ret_decay_bwd.astype(jnp.float32)))
    o_f = _retention_dir(q, k, v, lg_f, True)
    o_b = jnp.flip(_retention_dir(jnp.flip(q, 2), jnp.flip(k, 2), jnp.flip(v, 2), lg_b, False), 2)
    o = jnp.transpose(o_f + o_b, (0, 2, 1, 3))
    of = o.astype(jnp.float32)
    mu = jnp.mean(of, axis=-1, keepdims=True)
    var = jnp.mean(jnp.square(of - mu), axis=-1, keepdims=True)
    o = ((of - mu) * lax.rsqrt(var + GN_EPS)).astype(x.dtype).reshape(b, s, RET_V)
    o = o * ret_gn_w + ret_gn_b
    y_ret = (jax.nn.silu(g_ret) * o) @ w_ret_out

    ga, gb = jnp.split(jax.nn.sigmoid(gates + gate_b), 2, axis=-1)
    x = x + (ga * y_conv + gb * y_ret) @ w_mix_out

    h2 = _rmsnorm(x, norm_mlp)
    x = x + jnp.square(jax.nn.relu(h2 @ w_mlp_in)) @ w_mlp_out
    return x


def _trunk(x, norm_mix, w_in, conv_w, w_conv_out, ret_decay_fwd, ret_decay_bwd,
           ret_gn_w, ret_gn_b, w_ret_out, gate_b, w_mix_out, norm_mlp, w_mlp_in, w_mlp_out, norm_final):
    for l in range(DEPTH):
        x = _layer(x, norm_mix[l], w_in[l], conv_w[l], w_conv_out[l], ret_decay_fwd[l], ret_decay_bwd[l],
                   ret_gn_w[l], ret_gn_b[l], w_ret_out[l], gate_b[l], w_mix_out[l],
                   norm_mlp[l], w_mlp_in[l], w_mlp_out[l])
    return _rmsnorm(x, norm_final)


def setup_inputs(seed: int = 0) -> dict:
    key = jax.random.key(seed)
    ks = jax.random.split(key, 20)
    f32 = jnp.float32
    nrm = lambda k, shape, scale: jax.random.normal(k, shape, f32) * scale
    base_decay = math.log(2.0) * (-5.0 - jnp.arange(RET_HEADS, dtype=f32))
    return {
        "x_prompt": nrm(ks[0], (BATCH, SEQ, D_MODEL), 1.0),
        "x_sample": nrm(ks[1], (DEC_BATCH, DEC_SEQ, D_MODEL), 1.0),
        "norm_mix": 1.0 + nrm(ks[2], (DEPTH, D_MODEL), 0.02),
        "w_in": nrm(ks[3], (DEPTH, D_MODEL, IN_COLS), D_MODEL ** -0.5),
        "conv_w": nrm(ks[4], (DEPTH, CONV_K, CONV_W), CONV_K ** -0.5),
        "w_conv_out": nrm(ks[5], (DEPTH, CONV_W, D_MODEL), CONV_W ** -0.5),
        "ret_decay_fwd": base_decay[None] + nrm(ks[6], (DEPTH, RET_HEADS), 0.1),
        "ret_decay_bwd": base_decay[None] + nrm(ks[7], (DEPTH, RET_HEADS), 0.1),
        "ret_gn_w": 1.0 + nrm(ks[8], (DEPTH, RET_V), 0.02),
        "ret_gn_b": nrm(ks[9], (DEPTH, RET_V), 0.02),
        "w_ret_out": nrm(ks[10], (DEPTH, RET_V, D_MODEL), RET_V ** -0.5),
        "gate_b": nrm(ks[11], (DEPTH, N_BRANCH * D_MODEL), 0.02),
        "w_mix_out": nrm(ks[12], (DEPTH, D_MODEL, D_MODEL), D_MODEL ** -0.5),
        "norm_mlp": 1.0 + nrm(ks[13], (DEPTH, D_MODEL), 0.02),
        "w_mlp_in": nrm(ks[14], (DEPTH, D_MODEL, D_FF), D_MODEL ** -0.5),
        "w_mlp_out": nrm(ks[15], (DEPTH, D_FF, D_MODEL), D_FF ** -0.5),
        "norm_final": 1.0 + nrm(ks[16], (D_MODEL,), 0.02),
    }


def reference(x_prompt, x_sample, norm_mix, w_in, conv_w, w_conv_out, ret_decay_fwd, ret_decay_bwd,
              ret_gn_w, ret_gn_b, w_ret_out, gate_b, w_mix_out, norm_mlp, w_mlp_in, w_mlp_out, norm_final):
    y_prompt = _trunk(x_prompt, norm_mix, w_in, conv_w, w_conv_out, ret_decay_fwd, ret_decay_bwd,
                      ret_gn_w, ret_gn_b, w_ret_out, gate_b, w_mix_out, norm_mlp, w_mlp_in, w_mlp_out, norm_final)
    y_sample = _trunk(x_sample, norm_mix, w_in, conv_w, w_conv_out, ret_decay_fwd, ret_decay_bwd,
                      ret_gn_w, ret_gn_b, w_ret_out, gate_b, w_mix_out, norm_mlp, w_mlp_in, w_mlp_out, norm_final)
    return (y_prompt, y_sample)
```

```python
import math
from contextlib import ExitStack
import numpy as np
import concourse.bass as bass
import concourse.mybir as mybir
from concourse.bass_utils import run_bass_kernel_spmd

F32 = mybir.dt.float32
BF16 = mybir.dt.bfloat16
ALU = mybir.AluOpType
AF = mybir.ActivationFunctionType

D = 1024
KD = 8
H = 4
DK = 256
DV = 512
RV = 2048
DFF = 4096
INC = 11264
C = 128
TS = 512
NORM_EPS = 1e-6
GN_EPS = 1e-5
NU_L = 46
ARENA = 211968
BLK = 512
SEM_LIMIT = 28000
NDSEM = 16


class Buf:
    def __init__(self, ap, keys):
        self.ap = ap
        self.keys = keys

    def __getitem__(self, idx):
        return self.ap[idx]


class Stream:
    def __init__(self, name):
        self.name = name
        self.items = []
        self.sem = None
        self.cnt = 0
        self.waited = {}
        self.dsems = []
        self.dn = 0


class Tracker:
    def __init__(self, nc, es):
        self.nc = nc
        self.es = es
        self.streams = {n: Stream(n) for n in ("pe", "act", "dve", "pool", "sp")}
        self.state = {}
        self.nsem = 0
        for s in self.streams.values():
            s.sem = self._newsem(s.name)
        for n in ("sp", "pool", "act"):
            self.streams[n].dsems = [self._newsem(n + "d%d" % i) for i in range(NDSEM)]
        self.all_dma_events = []

    def _newsem(self, name):
        self.nsem += 1
        return self.es.enter_context(self.nc.semaphore("%s_%d" % (name, self.nsem)))

    def _wait(self, st, ev):
        sem, val = ev
        k = id(sem)
        if st.waited.get(k, 0) >= val:
            return
        st.waited[k] = val
        st.items.append(("w", sem, val))

    def op(self, stream, fn, r=(), w=(), dma=False, inc=True):
        st = self.streams[stream]
        compute = not dma
        deps = []
        for b in r:
            for k in b.keys:
                s = self.state.get(k)
                if s is not None and s[0] is not None:
                    deps.append(s[0])
        for b in w:
            for k in b.keys:
                s = self.state.get(k)
                if s is not None:
                    if s[0] is not None:
                        deps.append((s[0][0], s[0][1], "w"))
                    deps.extend((x[0], x[1], "r") for x in s[1])
        if dma:
            slot = st.dn % NDSEM
            rnd = st.dn // NDSEM
            st.dn += 1
            dsem = st.dsems[slot]
            if rnd > 0:
                deps.append((dsem, 16 * rnd))
            ev = (dsem, 16 * (rnd + 1))
            self.all_dma_events.append(ev)
        else:
            if st.cnt + 1 > SEM_LIMIT and not getattr(st, "in_group", False):
                st.sem = self._newsem(st.name)
                st.cnt = 0
            st.in_group = not inc
            ev = (st.sem, st.cnt + 1)
            if inc:
                st.cnt += 1
        for d in deps:
            sem, val = d[0], d[1]
            if compute and sem is st.sem:
                if stream == "pe" or len(d) == 3:
                    continue
                if val > st.cnt - (1 if inc else 0):
                    continue
            self._wait(st, (sem, val))
        if dma:
            st.items.append(("d", fn, ev[0]))
        else:
            st.items.append(("c", fn, ev[0] if inc else None))
        for b in r:
            for k in b.keys:
                s = self.state.get(k)
                if s is None:
                    s = [None, []]
                    self.state[k] = s
                s[1].append(ev)
        for b in w:
            for k in b.keys:
                self.state[k] = [ev, []]
        return ev

    def finish(self):
        st = self.streams["sp"]
        last = {}
        for ev in self.all_dma_events:
            last[id(ev[0])] = ev
        for ev in last.values():
            self._wait(st, ev)
        for n, s in self.streams.items():
            if n != "sp" and s.cnt > 0:
                self._wait(st, (s.sem, s.cnt))

    def replay(self, name, eng):
        for it in self.streams[name].items:
            if it[0] == "w":
                eng.wait_ge(it[1], it[2])
            elif it[0] == "d":
                it[1](eng).then_inc(it[2], 16)
            else:
                ins = it[1](eng)
                if it[2] is not None:
                    ins.then_inc(it[2], 1)


class Arena:
    def __init__(self, tensor):
        self.t = tensor
        self.off = 0

    def reset(self, off=0):
        self.off = off

    def alloc(self, shape, dtype):
        es = 2 if dtype == BF16 else 4
        n = int(np.prod(shape))
        nbytes = n * es
        start = (self.off + BLK - 1) // BLK * BLK
        end = start + nbytes
        assert end <= ARENA, ("arena overflow", end)
        self.off = end
        ap = self.t[:, start // 2:(start + nbytes) // 2]
        if dtype != BF16:
            ap = ap.bitcast(dtype)
        if len(shape) == 2:
            ap = ap.rearrange("p (a b) -> p a b", a=shape[0])
        elif len(shape) == 3:
            ap = ap.rearrange("p (a b c) -> p a b c", a=shape[0], b=shape[1])
        keys = [("sb", b) for b in range(start // BLK, (end + BLK - 1) // BLK)]
        return Buf(ap, keys)


def build_program(seq_lens, depth, smax):
    nc = bass.Bass("TRN2", target_bir_lowering=False)
    nseq = len(seq_lens)

    def din(name, shape, dt=F32):
        return nc.dram_tensor(name, list(shape), dt, kind="ExternalInput").ap()

    def dscr(name, shape, dt):
        return nc.dram_tensor(name, list(shape), dt, kind="Internal").ap()

    xin = [din("x%d" % i, [s, D]) for i, s in enumerate(seq_lens)]
    yout = [nc.dram_tensor("y%d" % i, [s, D], F32, kind="ExternalOutput").ap() for i, s in enumerate(seq_lens)]
    w_in = din("w_in", [depth, D, INC])
    w_co = din("w_conv_out", [depth, D, D])
    w_ro = din("w_ret_out", [depth, RV, D])
    w_mx = din("w_mix_out", [depth, D, D])
    w_mi = din("w_mlp_in", [depth, D, DFF])
    w_mo = din("w_mlp_out", [depth, DFF, D])
    norm_mix = din("norm_mix", [depth, D])
    norm_mlp = din("norm_mlp", [depth, D])
    norm_final = din("norm_final", [1, D])
    conv_w = din("conv_w", [depth, 3, D])
    dec_f = din("ret_decay_fwd", [depth, H])
    dec_b = din("ret_decay_bwd", [depth, H])
    gn_w = din("ret_gn_w", [depth, RV])
    gn_b = din("ret_gn_b", [depth, RV])
    gate_b = din("gate_b", [depth, 2 * D])
    costab = din("costab", [128, smax])
    sintab = din("sintab", [128, smax])
    consts = din("consts", [128, 6 * 128 + 2])

    wq = dscr("wq", [depth * NU_L, 128, 4096], BF16)
    scr = []
    for i, s in enumerate(seq_lens):
        nch = s // C
        scr.append(dict(
            zT=dscr("zT%d" % i, [8, 128, s + 2], F32),
            cbT=dscr("cbT%d" % i, [8, 128, s], BF16),
            gaT=dscr("gaT%d" % i, [16, 128, s], BF16),
            qT=dscr("qT%d" % i, [nch, 128, 8 * 128], BF16),
            kT=dscr("kT%d" % i, [nch, 128, 8 * 128], BF16),
            v=dscr("v%d" % i, [s, RV], BF16),
            sgw=dscr("sgw%d" % i, [s, RV], BF16),
            sgb=dscr("sgb%d" % i, [s, RV], BF16),
            sb=dscr("sb%d" % i, [nch, 128, 8 * 512], BF16),
            gT=dscr("gT%d" % i, [nch, 128, 16 * 128], BF16),
            xa=dscr("xa%d" % i, [s, D], F32),
            xb=dscr("xb%d" % i, [s, D], F32),
        ))

    es = ExitStack()
    with es:
        arena_t = es.enter_context(nc.sbuf_tensor("arena", [128, ARENA // 2], BF16))
        A = Arena(arena_t)
        psb = [es.enter_context(nc.psum_tensor("ps%d" % i, [128, 512], F32)) for i in range(8)]
        T = Tracker(nc, es)
        ps_bufs = [Buf(psb[i][:, :], [("ps", i)]) for i in range(8)]
        ps_ctr = [0]

        def bank():
            b = ps_bufs[ps_ctr[0] % 8]
            ps_ctr[0] += 1
            return b

        def dkey(name, *idx):
            return Buf(None, [("dram", name) + tuple(idx)])

        ident_f = A.alloc([128], F32)
        ident = A.alloc([128], BF16)
        cst = A.alloc([6 * 128 + 2], F32)
        neghalf = A.alloc([8], F32)
        gmix = A.alloc([KD], F32)
        gmlp = A.alloc([KD], F32)
        cw = A.alloc([3, 8], F32)
        gbias = A.alloc([16], F32)
        dcy = A.alloc([8], F32)
        lg = A.alloc([8], F32)
        kdec = A.alloc([8], F32)
        cdec = A.alloc([8], F32)
        base_off = A.off

        cA = lambda: cst[:, 0:128]
        cB = lambda: cst[:, 128:256]
        cMf = lambda: cst[:, 256:384]
        cMb = lambda: cst[:, 384:512]
        cNp1 = lambda: cst[:, 512:640]
        cCmn = lambda: cst[:, 640:768]
        cP = lambda: cst[:, 768:769]
        cRp = lambda: cst[:, 769:770]

        T.op("sp", lambda e: e.dma_start(out=cst.ap, in_=consts[:, :]), w=[cst], dma=True)
        T.op("pool", lambda e: e.memset(neghalf.ap, -0.5), w=[neghalf])
        T.op("pool", lambda e: e.memset(ident_f.ap, 1.0), w=[ident_f])
        T.op("pool", lambda e: e.affine_select(out=ident_f.ap, in_=ident_f.ap, pattern=[[-1, 128]],
                                               compare_op=ALU.is_equal, fill=0.0, base=0, channel_multiplier=1),
             r=[ident_f], w=[ident_f])
        T.op("dve", lambda e: e.tensor_copy(out=ident.ap, in_=ident_f.ap), r=[ident_f], w=[ident])

        def unit_src(l, u):
            if u < 22:
                return w_in[l].rearrange("(k p) c -> p k c", p=128)[:, :, u * 512:(u + 1) * 512]
            if u < 24:
                return w_co[l].rearrange("(k p) c -> p k c", p=128)[:, :, (u - 22) * 512:(u - 21) * 512]
            if u < 28:
                kh, ch = (u - 24) // 2, (u - 24) % 2
                return w_ro[l].rearrange("(k p) c -> p k c", p=128)[:, kh * 8:(kh + 1) * 8, ch * 512:(ch + 1) * 512]
            if u < 30:
                return w_mx[l].rearrange("(k p) c -> p k c", p=128)[:, :, (u - 28) * 512:(u - 27) * 512]
            if u < 38:
                return w_mi[l].rearrange("(k p) c -> p k c", p=128)[:, :, (u - 30) * 512:(u - 29) * 512]
            kq, ch = (u - 38) // 2, (u - 38) % 2
            return w_mo[l].rearrange("(k p) c -> p k c", p=128)[:, kq * 8:(kq + 1) * 8, ch * 512:(ch + 1) * 512]

        for l in range(depth):
            for u in range(NU_L):
                dst = wq[l * NU_L + u].rearrange("p (k c) -> p k c", k=8)
                src = unit_src(l, u)
                T.op("pool", lambda e, dst=dst, src=src: e.dma_start(out=dst, in_=src),
                     w=[dkey("wq", l * NU_L + u)], dma=True)

        zcol = A.alloc([8, 1], F32)
        base_off = A.off
        T.op("pool", lambda e: e.memset(zcol.ap, 0.0), w=[zcol])
        for i, s in enumerate(seq_lens):
            for col in (0, s + 1):
                T.op("sp", lambda e, i=i, col=col: e.dma_start(
                    out=scr[i]["zT"].rearrange("j p s -> p j s")[:, :, col:col + 1], in_=zcol.ap, allow_slow_non_contiguous=True),
                    r=[zcol], w=[dkey("zTh", i, col)], dma=True)

        class Ring:
            def __init__(self, nslots, hold):
                self.slots = [A.alloc([8, 512], BF16) for _ in range(nslots)]
                self.hold = hold
                self.cur = {}
                self.n = 0
                self.plan = []
                self.loaded = 0
                self.used = 0

            def set_plan(self, units):
                self.plan = list(units)
                self.loaded = 0
                self.used = 0

            def _load_one(self):
                u = self.plan[self.loaded]
                slot = self.slots[self.n % len(self.slots)]
                self.n += 1
                self.loaded += 1
                T.op("sp", lambda e, slot=slot, u=u: e.dma_start(
                    out=slot.ap, in_=wq[u].rearrange("p (k c) -> p k c", k=8)),
                    r=[dkey("wq", u)], w=[slot], dma=True)
                return slot

            def prefetch(self):
                ahead = len(self.slots) - self.hold + 1
                while self.loaded < len(self.plan) and self.loaded < self.used + ahead:
                    idx = self.loaded
                    self.cur[idx] = self._load_one()

            def get(self, idx):
                self.used = max(self.used, idx)
                self.prefetch()
                return self.cur[idx]

        def rstd_from_ss(ss, rstd, n, scale, eps):
            T.op("dve", lambda e: e.tensor_scalar(out=ss.ap[:, 0:n], in0=ss.ap[:, 0:n], scalar1=scale, scalar2=eps,
                                                  op0=ALU.mult, op1=ALU.add), r=[ss], w=[ss])
            T.op("pool", lambda e: e.tensor_tensor(out=rstd.ap[:, 0:n], in0=ss.ap[:, 0:n], in1=neghalf.ap[:, 0:n],
                                                   op=ALU.pow), r=[ss, neghalf], w=[rstd])

        def layer_consts(l):
            T.op("sp", lambda e: e.dma_start(out=cw.ap, in_=conv_w[l].rearrange("t (j p) -> p t j", p=128),
                                             allow_slow_non_contiguous=True), w=[cw], dma=True)
            T.op("sp", lambda e: e.dma_start(out=gbias.ap, in_=gate_b[l].rearrange("(j p) -> p j", p=128),
                                             allow_slow_non_contiguous=True), w=[gbias], dma=True)
            T.op("sp", lambda e: e.dma_start(out=dcy.ap[:, 0:4], in_=dec_f[l:l + 1, :].partition_broadcast(128)[:, 0, :]),
                 w=[dcy], dma=True)
            T.op("sp", lambda e: e.dma_start(out=dcy.ap[:, 4:8], in_=dec_b[l:l + 1, :].partition_broadcast(128)[:, 0, :]),
                 w=[dcy], dma=True)
            T.op("act", lambda e: e.activation(out=lg.ap, in_=dcy.ap, func=AF.Exp), r=[dcy], w=[lg])
            T.op("dve", lambda e: e.tensor_scalar(out=lg.ap, in0=lg.ap, scalar1=-1.0, scalar2=1.0, op0=ALU.mult, op1=ALU.add), r=[lg], w=[lg])
            T.op("act", lambda e: e.activation(out=lg.ap, in_=lg.ap, func=AF.Ln), r=[lg], w=[lg])
            for h in range(H):
                T.op("act", lambda e, h=h: e.activation(out=kdec.ap[:, h:h + 1], in_=cRp(), func=AF.Exp, scale=lg.ap[:, h:h + 1]),
                     r=[cst, lg], w=[kdec])
                T.op("act", lambda e, h=h: e.activation(out=kdec.ap[:, 4 + h:5 + h], in_=cP(), func=AF.Exp, scale=lg.ap[:, 4 + h:5 + h]),
                     r=[cst, lg], w=[kdec])
            T.op("dve", lambda e: e.tensor_scalar(out=kdec.ap, in0=kdec.ap, scalar1=0.0625, scalar2=None, op0=ALU.mult),
                 r=[kdec], w=[kdec])
            T.op("act", lambda e: e.activation(out=cdec.ap, in_=lg.ap, func=AF.Exp, scale=float(C)), r=[lg], w=[cdec])


        def ret_consts(l, DT, qdF, qdB, tmpc):
            for h in range(H):
                T.op("act", lambda e, h=h: e.activation(out=tmpc.ap[:, 0, :], in_=cA(), func=AF.Exp, scale=lg.ap[:, h:h + 1]),
                     r=[cst, lg], w=[tmpc])
                T.op("act", lambda e, h=h: e.activation(out=tmpc.ap[:, 1, :], in_=cB(), func=AF.Exp, scale=lg.ap[:, 4 + h:5 + h]),
                     r=[cst, lg], w=[tmpc])
                T.op("dve", lambda e: e.tensor_tensor(out=tmpc.ap[:, 0, :], in0=tmpc.ap[:, 0, :], in1=cMf(), op=ALU.mult),
                     r=[tmpc, cst], w=[tmpc])
                T.op("dve", lambda e: e.tensor_tensor(out=tmpc.ap[:, 1, :], in0=tmpc.ap[:, 1, :], in1=cMb(), op=ALU.mult),
                     r=[tmpc, cst], w=[tmpc])
                T.op("dve", lambda e, h=h: e.tensor_tensor(out=DT.ap[:, h, :], in0=tmpc.ap[:, 0, :], in1=tmpc.ap[:, 1, :], op=ALU.add),
                     r=[tmpc], w=[DT])
                for dc in range(2):
                    T.op("act", lambda e, h=h, dc=dc: e.activation(out=qdF.ap[:, 2 * h + dc, :], in_=cNp1(), func=AF.Exp,
                                                                   scale=lg.ap[:, h:h + 1]), r=[cst, lg], w=[qdF])
                    T.op("act", lambda e, h=h, dc=dc: e.activation(out=qdB.ap[:, 2 * h + dc, :], in_=cCmn(), func=AF.Exp,
                                                                   scale=lg.ap[:, 4 + h:5 + h]), r=[cst, lg], w=[qdB])

        def subs(whole, n):
            nb = len(whole.keys) // n
            assert nb * n == len(whole.keys)
            return [Buf(whole.ap[:, j], whole.keys[j * nb:(j + 1) * nb]) for j in range(n)]

        def norm_a(x_ap, xbuf, gtab, xs, ss, rstd, junk):
            T.op("act", lambda e: e.activation(out=xs.ap, in_=x_ap, func=AF.Square, accum_out=ss.ap[:, 0:1]),
                 r=[xbuf], w=[xs, ss])
            rstd_from_ss(ss, rstd, 1, 1.0 / D, NORM_EPS)
            T.op("dve", lambda e: e.scalar_tensor_tensor(out=xs.ap, in0=x_ap, scalar=rstd.ap[:, 0:1], in1=gtab.ap,
                                                         op0=ALU.mult, op1=ALU.mult), r=[xbuf, rstd, gtab], w=[xs])

        def norm_b(xs, hT, c):
            pb = bank()
            pv = pb.ap.bitcast(BF16)
            for k in range(KD):
                T.op("pe", lambda e, k=k: e.transpose(out=pv[:, k * 128:(k + 1) * 128], in_=xs.ap[:, k * 128:(k + 1) * 128],
                                                      identity=ident.ap), r=[xs, ident], w=[pb], inc=(k == KD - 1))
            T.op("act", lambda e: e.activation(out=hT.ap[:, :, c * 128:(c + 1) * 128], in_=pv.rearrange("p (k n) -> p k n", k=KD),
                                               func=AF.Copy), r=[pb], w=[hT])

        def mm_group(pb, out_ap, pairs, r):
            n = len(pairs)
            for i, (l_ap, r_ap) in enumerate(pairs):
                T.op("pe", lambda e, l_ap=l_ap, r_ap=r_ap, i=i: e.matmul(out_ap, lhsT=l_ap, rhs=r_ap, start=(i == 0), stop=(i == n - 1)),
                     r=r, w=[pb], inc=(i == n - 1))

        def sweep1(l, i, xsrc, xsrc_name):
            S = seq_lens[i]
            sc = scr[i]
            A.reset(base_off)
            ring = Ring(6, 3)
            xt = [A.alloc([D], F32) for _ in range(2)]
            xs = [A.alloc([D], BF16) for _ in range(4)]
            junk = A.alloc([D], BF16)
            ss = [A.alloc([1], F32) for _ in range(4)]
            rstd = [A.alloc([1], F32) for _ in range(4)]
            hTs = [A.alloc([KD, TS], BF16) for _ in range(2)]
            gtab = A.alloc([D], F32)
            gnw = A.alloc([RV], F32)
            gnb = A.alloc([RV], F32)
            ccj = [A.alloc([TS], F32) for _ in range(2)]
            zt = [A.alloc([4, TS], F32) for _ in range(2)]
            cbt = [A.alloc([4, TS], BF16) for _ in range(2)]
            qt = A.alloc([8, TS], BF16)
            kt = A.alloc([8, TS], BF16)
            vt = [A.alloc([4, 512], BF16) for _ in range(2)]
            sgwt = [A.alloc([4, 512], BF16) for _ in range(2)]
            sgbt = [A.alloc([4, 512], BF16) for _ in range(2)]
            gat = [A.alloc([4, TS], BF16) for _ in range(2)]
            rt = [A.alloc([TS], F32) for _ in range(4)]
            sig = [rt[0], rt[1]]
            sgf = [rt[2], rt[3]]
            cs = A.alloc([2, TS], F32)
            T.op("sp", lambda e: e.dma_start(out=gtab.ap, in_=norm_mix[l:l + 1, :].partition_broadcast(128)[:, 0, :]), w=[gtab], dma=True)
            T.op("sp", lambda e: e.dma_start(out=gnw.ap, in_=gn_w[l:l + 1, :].partition_broadcast(128)[:, 0, :]), w=[gnw], dma=True)
            T.op("sp", lambda e: e.dma_start(out=gnb.ap, in_=gn_b[l:l + 1, :].partition_broadcast(128)[:, 0, :]), w=[gnb], dma=True)
            order = [2, 4, 0, 3, 5, 1] + list(range(6, 22))
            ntile = S // TS
            plan = []
            for t in range(ntile):
                plan += [l * NU_L + u for u in order]
            ring.set_plan(plan)
            cnt = [0]

            def front_a(t):
                for c in range(4):
                    r0 = t * TS + c * C
                    T.op("sp", lambda e, c=c, r0=r0: e.dma_start(out=xt[c % 2].ap, in_=xsrc[r0:r0 + C, :]),
                         r=[dkey(xsrc_name, i, r0 // C)], w=[xt[c % 2]], dma=True)
                    norm_a(xt[c % 2].ap, xt[c % 2], gtab, xs[c], ss[c], rstd[c], junk)

            def front_b(t):
                for c in range(4):
                    norm_b(xs[c], hTs[t % 2], c)

            front_a(0)
            front_b(0)
            for t in range(ntile):
                t0 = t * TS
                base = t * 22
                hT = hTs[t % 2]
                ring.prefetch()
                T.op("sp", lambda e, t0=t0: e.dma_start(out=cs.ap[:, 0, :], in_=costab[:, t0:t0 + TS]), w=[cs], dma=True)
                T.op("sp", lambda e, t0=t0: e.dma_start(out=cs.ap[:, 1, :], in_=sintab[:, t0:t0 + TS]), w=[cs], dma=True)

                def fm(pos, jj):
                    wslot = ring.get(base + pos)
                    pb = bank()
                    mm_group(pb, pb.ap, [(wslot.ap[:, k, jj * 128:(jj + 1) * 128], hT.ap[:, k, :]) for k in range(KD)], [wslot, hT])
                    return pb

                def tm(pos, c):
                    wslot = ring.get(base + pos)
                    pb = bank()
                    mm_group(pb, pb.ap, [(hT.ap[:, k, c * 128:(c + 1) * 128], wslot.ap[:, k, :]) for k in range(KD)], [wslot, hT])
                    return pb

                for half in range(2):
                    ztb, cbb = zt[half], cbt[half]
                    for jj in range(4):
                        j = half * 4 + jj
                        pcc = fm(half * 3 + 0, jj)
                        cj = ccj[j % 2]
                        T.op("act", lambda e, pcc=pcc, cj=cj: e.activation(out=cj.ap, in_=pcc.ap, func=AF.Copy), r=[pcc], w=[cj])
                        pcx = fm(half * 3 + 1, jj)
                        T.op("dve", lambda e, pcx=pcx, cj=cj, jj=jj, ztb=ztb: e.tensor_tensor(out=ztb.ap[:, jj, :], in0=pcx.ap, in1=cj.ap, op=ALU.mult),
                             r=[pcx, cj], w=[ztb])
                        pcb = fm(half * 3 + 2, jj)
                        T.op("act", lambda e, pcb=pcb, jj=jj, cbb=cbb: e.activation(out=cbb.ap[:, jj, :], in_=pcb.ap, func=AF.Copy), r=[pcb], w=[cbb])
                    T.op("act", lambda e, t0=t0, half=half, ztb=ztb: e.dma_start(
                        out=sc["zT"].rearrange("j p s -> p j s")[:, half * 4:half * 4 + 4, 1 + t0:1 + t0 + TS], in_=ztb.ap),
                        r=[ztb], w=[dkey("zT", i, t, half)], dma=True)
                    T.op("act", lambda e, t0=t0, half=half, cbb=cbb: e.dma_start(
                        out=sc["cbT"].rearrange("j p s -> p j s")[:, half * 4:half * 4 + 4, t0:t0 + TS], in_=cbb.ap),
                        r=[cbb], w=[dkey("cbT", i, t, half)], dma=True)
                if t + 1 < ntile:
                    front_a(t + 1)
                for which, dst, dname in ((0, qt, "qT"), (1, kt, "kT")):
                    for uu in range(2):
                        for hh in range(2):
                            h = uu * 2 + hh
                            p0 = fm(6 + which * 2 + uu, hh * 2)
                            p1 = fm(6 + which * 2 + uu, hh * 2 + 1)
                            T.op("dve", lambda e, p0=p0: e.tensor_tensor(out=rt[0].ap, in0=p0.ap, in1=cs.ap[:, 0, :], op=ALU.mult), r=[p0, cs], w=[rt[0]])
                            T.op("dve", lambda e, p1=p1: e.tensor_tensor(out=rt[1].ap, in0=p1.ap, in1=cs.ap[:, 1, :], op=ALU.mult), r=[p1, cs], w=[rt[1]])
                            T.op("pool", lambda e, dst=dst, h=h: e.tensor_tensor(out=dst.ap[:, 2 * h, :], in0=rt[0].ap, in1=rt[1].ap, op=ALU.subtract),
                                 r=[rt[0], rt[1]], w=[dst])
                            T.op("dve", lambda e, p0=p0: e.tensor_tensor(out=rt[2].ap, in0=p0.ap, in1=cs.ap[:, 1, :], op=ALU.mult), r=[p0, cs], w=[rt[2]])
                            T.op("dve", lambda e, p1=p1: e.tensor_tensor(out=rt[3].ap, in0=p1.ap, in1=cs.ap[:, 0, :], op=ALU.mult), r=[p1, cs], w=[rt[3]])
                            T.op("pool", lambda e, dst=dst, h=h: e.tensor_tensor(out=dst.ap[:, 2 * h + 1, :], in0=rt[2].ap, in1=rt[3].ap, op=ALU.add),
                                 r=[rt[2], rt[3]], w=[dst])
                    for c in range(4):
                        ci = t * 4 + c
                        T.op("act", lambda e, dst=dst, dname=dname, ci=ci, c=c: e.dma_start(
                            out=sc[dname][ci].rearrange("p (j n) -> p j n", j=8), in_=dst.ap[:, :, c * 128:(c + 1) * 128]),
                            r=[dst], w=[dkey(dname, i, ci)], dma=True)
                if t + 1 < ntile:
                    front_b(t + 1)
                for h in range(H):
                    vb = vt[h % 2]
                    for c in range(4):
                        pv_ = tm(10 + h, c)
                        if c % 2 == 0:
                            T.op("act", lambda e, pv_=pv_, c=c, vb=vb: e.activation(out=vb.ap[:, c, :], in_=pv_.ap, func=AF.Copy), r=[pv_], w=[vb])
                        else:
                            T.op("dve", lambda e, pv_=pv_, c=c, vb=vb: e.tensor_copy(out=vb.ap[:, c, :], in_=pv_.ap), r=[pv_], w=[vb])
                    T.op("act", lambda e, t0=t0, h=h, vb=vb: e.dma_start(
                        out=sc["v"][t0:t0 + TS, h * 512:(h + 1) * 512].rearrange("(c p) e -> p c e", p=128), in_=vb.ap),
                        r=[vb], w=[dkey("v", i, t, h)], dma=True)
                for h in range(H):
                    wb, bb = sgwt[h % 2], sgbt[h % 2]
                    for c in range(4):
                        pg = tm(14 + h, c)
                        sb_ = sig[(h * 4 + c) % 2]
                        sf_ = sgf[(h * 4 + c) % 2]
                        T.op("act", lambda e, pg=pg, sb_=sb_: e.activation(out=sb_.ap, in_=pg.ap, func=AF.Sigmoid), r=[pg], w=[sb_])
                        T.op("dve", lambda e, pg=pg, sb_=sb_, sf_=sf_: e.tensor_tensor(out=sf_.ap, in0=pg.ap, in1=sb_.ap, op=ALU.mult),
                             r=[pg, sb_], w=[sf_])
                        T.op("dve", lambda e, sf_=sf_, c=c, h=h, wb=wb: e.tensor_tensor(out=wb.ap[:, c, :], in0=sf_.ap, in1=gnw.ap[:, h * 512:(h + 1) * 512], op=ALU.mult),
                             r=[sf_, gnw], w=[wb])
                        T.op("pool", lambda e, sf_=sf_, c=c, h=h, bb=bb: e.tensor_tensor(out=bb.ap[:, c, :], in0=sf_.ap, in1=gnb.ap[:, h * 512:(h + 1) * 512], op=ALU.mult),
                             r=[sf_, gnb], w=[bb])
                    T.op("act", lambda e, t0=t0, h=h, wb=wb: e.dma_start(
                        out=sc["sgw"][t0:t0 + TS, h * 512:(h + 1) * 512].rearrange("(c p) e -> p c e", p=128), in_=wb.ap),
                        r=[wb], w=[dkey("sgw", i, t, h)], dma=True)
                    T.op("act", lambda e, t0=t0, h=h, bb=bb: e.dma_start(
                        out=sc["sgb"][t0:t0 + TS, h * 512:(h + 1) * 512].rearrange("(c p) e -> p c e", p=128), in_=bb.ap),
                        r=[bb], w=[dkey("sgb", i, t, h)], dma=True)
                for uu in range(4):
                    gb_ = gat[uu % 2]
                    for jj in range(4):
                        f = uu * 4 + jj
                        pg = fm(18 + uu, jj)
                        T.op("act", lambda e, pg=pg, f=f, jj=jj, gb_=gb_: e.activation(out=gb_.ap[:, jj, :], in_=pg.ap, func=AF.Sigmoid, bias=gbias.ap[:, f:f + 1]),
                             r=[pg, gbias], w=[gb_])
                    T.op("act", lambda e, t0=t0, uu=uu, gb_=gb_: e.dma_start(
                        out=sc["gaT"].rearrange("j p s -> p j s")[:, uu * 4:uu * 4 + 4, t0:t0 + TS], in_=gb_.ap),
                        r=[gb_], w=[dkey("gaT", i, t, uu)], dma=True)

        def kdec_transposes(ktc, Kx, doff, pb):
            pv = pb.ap.bitcast(BF16)
            for j in range(8):
                T.op("pe", lambda e, j=j: e.transpose(out=pv[:, j * 128:(j + 1) * 128], in_=ktc.ap[:, j, :], identity=ident.ap),
                     r=[ktc, ident], w=[pb], inc=(j == 7))
            for h in range(H):
                T.op("act", lambda e, h=h: e.activation(out=Kx.ap[:, h * 256:(h + 1) * 256], in_=pv[:, h * 256:(h + 1) * 256],
                                                        func=AF.Copy, scale=kdec.ap[:, doff + h:doff + h + 1]), r=[pb, kdec], w=[Kx])

        def state_update(Kx, vch, S32in, S32out, S16n, doff, banks):
            for h in range(H):
                for dc in range(2):
                    j = 2 * h + dc
                    pb = banks[j % len(banks)]
                    mm_group(pb, pb.ap, [(Kx.ap[:, j * 128:(j + 1) * 128], vch.ap[:, h * 512:(h + 1) * 512])], [Kx, vch])
                    T.op("dve", lambda e, pb=pb, j=j, h=h: e.scalar_tensor_tensor(
                        out=S32out[j].ap, in0=S32in[j].ap, scalar=cdec.ap[:, doff + h:doff + h + 1], in1=pb.ap,
                        op0=ALU.mult, op1=ALU.add), r=[S32in[j], cdec, pb], w=[S32out[j]])
                    if S16n is not None:
                        T.op("act", lambda e, j=j: e.activation(out=S16n[j].ap, in_=S32out[j].ap, func=AF.Copy), r=[S32out[j]], w=[S16n[j]])

        def sweep2a(l, i):
            S = seq_lens[i]
            sc = scr[i]
            nch = S // C
            A.reset(base_off)
            ktc = [A.alloc([8, 128], BF16) for _ in range(2)]
            vch = [A.alloc([RV], BF16) for _ in range(2)]
            Kx = [A.alloc([D], BF16) for _ in range(2)]
            S32w = [A.alloc([8, 512], F32) for _ in range(2)]
            S32 = [subs(w_, 8) for w_ in S32w]
            T.op("pool", lambda e: e.memset(S32w[0].ap, 0.0), w=[S32w[0]])
            for n_, ci in enumerate(range(nch - 1, -1, -1)):
                b = n_ % 2
                T.op("sp", lambda e, ci=ci, b=b: e.dma_start(out=ktc[b].ap, in_=sc["kT"][ci].rearrange("p (j n) -> p j n", j=8)),
                     r=[dkey("kT", i, ci)], w=[ktc[b]], dma=True)
                T.op("sp", lambda e, ci=ci, b=b: e.dma_start(out=vch[b].ap, in_=sc["v"][ci * C:(ci + 1) * C, :]),
                     r=[dkey("v", i, ci // 4, h) for h in range(H)], w=[vch[b]], dma=True)
                T.op("pool", lambda e, ci=ci, b=b: e.dma_start(out=sc["sb"][ci].rearrange("p (j n) -> p j n", j=8), in_=S32w[b].ap),
                     r=[S32w[b]], w=[dkey("sb", i, ci)], dma=True)
                if ci > 0:
                    kdec_transposes(ktc[b], Kx[b], 4, bank())
                    state_update(Kx[b], vch[b], S32[b], S32[1 - b], None, 4, [bank() for _ in range(4)])

        def sweep2b(l, i):
            S = seq_lens[i]
            sc = scr[i]
            nch = S // C
            A.reset(base_off)
            qtc = [A.alloc([8, 128], BF16) for _ in range(2)]
            ktc = [A.alloc([8, 128], BF16) for _ in range(2)]
            vch = [A.alloc([RV], BF16) for _ in range(2)]
            sgwc = [A.alloc([4, 512], BF16) for _ in range(2)]
            sgbc = [A.alloc([4, 512], BF16) for _ in range(2)]
            sbc = [A.alloc([8, 512], BF16) for _ in range(2)]
            Kx = [A.alloc([D], BF16) for _ in range(2)]
            S32w = A.alloc([8, 512], F32)
            S16w = [A.alloc([8, 512], BF16) for _ in range(2)]
            S32 = subs(S32w, 8)
            S16 = [subs(w_, 8) for w_ in S16w]
            Qf = [A.alloc([8, 128], BF16) for _ in range(2)]
            Qb = [A.alloc([8, 128], BF16) for _ in range(2)]
            PT = [A.alloc([512], BF16) for _ in range(2)]
            st6 = [A.alloc([H, 6], F32) for _ in range(2)]
            mv = [A.alloc([H, 2], F32) for _ in range(2)]
            ve = [A.alloc([H], F32) for _ in range(2)]
            rs4 = [A.alloc([H], F32) for _ in range(2)]
            nb4 = [A.alloc([H], F32) for _ in range(2)]
            on = subs(A.alloc([H, 512], F32), H)
            t1 = subs(A.alloc([H, 512], F32), H)
            gatedw = [A.alloc([H, 512], BF16) for _ in range(2)]
            gated = [subs(w_, H) for w_ in gatedw]
            gTt = [A.alloc([16, 128], BF16) for _ in range(2)]
            DT = A.alloc([H, 128], F32)
            qdF = A.alloc([8, 128], F32)
            qdB = A.alloc([8, 128], F32)
            tmpc = A.alloc([2, 128], F32)
            ret_consts(l, DT, qdF, qdB, tmpc)
            T.op("pool", lambda e: e.memset(S32w.ap, 0.0), w=[S32w])
            T.op("pool", lambda e: e.memset(S16w[0].ap, 0.0), w=[S16w[0]])
            P = ps_bufs

            def loads_a(ci):
                b = ci % 2
                T.op("sp", lambda e: e.dma_start(out=qtc[b].ap, in_=sc["qT"][ci].rearrange("p (j n) -> p j n", j=8)),
                     r=[dkey("qT", i, ci)], w=[qtc[b]], dma=True)
                T.op("sp", lambda e: e.dma_start(out=ktc[b].ap, in_=sc["kT"][ci].rearrange("p (j n) -> p j n", j=8)),
                     r=[dkey("kT", i, ci)], w=[ktc[b]], dma=True)
                T.op("sp", lambda e: e.dma_start(out=vch[b].ap, in_=sc["v"][ci * C:(ci + 1) * C, :]),
                     r=[dkey("v", i, ci // 4, h) for h in range(H)], w=[vch[b]], dma=True)
                T.op("sp", lambda e: e.dma_start(out=sbc[b].ap, in_=sc["sb"][ci].rearrange("p (j n) -> p j n", j=8)),
                     r=[dkey("sb", i, ci)], w=[sbc[b]], dma=True)

            def loads_g(ci):
                b = ci % 2
                T.op("sp", lambda e: e.dma_start(out=sgwc[b].ap, in_=sc["sgw"][ci * C:(ci + 1) * C, :].rearrange("p (h e) -> p h e", h=H)),
                     r=[dkey("sgw", i, ci // 4, h) for h in range(H)], w=[sgwc[b]], dma=True)
                T.op("sp", lambda e: e.dma_start(out=sgbc[b].ap, in_=sc["sgb"][ci * C:(ci + 1) * C, :].rearrange("p (h e) -> p h e", h=H)),
                     r=[dkey("sgb", i, ci // 4, h) for h in range(H)], w=[sgbc[b]], dma=True)

            def stage_q(ci):
                b = ci % 2
                T.op("pool", lambda e: e.tensor_tensor(out=Qf[b].ap, in0=qtc[b].ap, in1=qdF.ap, op=ALU.mult), r=[qtc[b], qdF], w=[Qf[b]])
                T.op("pool", lambda e: e.tensor_tensor(out=Qb[b].ap, in0=qtc[b].ap, in1=qdB.ap, op=ALU.mult), r=[qtc[b], qdB], w=[Qb[b]])

            def stage_s(ci):
                b = ci % 2
                psS = P[4]
                for h in range(H):
                    mm_group(psS, psS.ap[:, h * 128:(h + 1) * 128],
                             [(ktc[b].ap[:, 2 * h + dc, :], qtc[b].ap[:, 2 * h + dc, :]) for dc in range(2)], [ktc[b], qtc[b]])
                T.op("dve", lambda e: e.tensor_tensor(out=PT[b].ap, in0=psS.ap, in1=DT.ap.rearrange("p h n -> p (h n)"), op=ALU.mult),
                     r=[psS, DT], w=[PT[b]])

            def stage_o(ci, upd):
                b = ci % 2
                for h in range(H):
                    po = P[h]
                    pairs = [(PT[b].ap[:, h * 128:(h + 1) * 128], vch[b].ap[:, h * 512:(h + 1) * 512])]
                    rr = [PT[b], vch[b], Qf[b], Qb[b], sbc[b]]
                    for dc in range(2):
                        pairs.append((Qf[b].ap[:, 2 * h + dc, :], S16[b][2 * h + dc].ap))
                        rr.append(S16[b][2 * h + dc])
                    for dc in range(2):
                        pairs.append((Qb[b].ap[:, 2 * h + dc, :], sbc[b].ap[:, 2 * h + dc, :]))
                    mm_group(po, po.ap, pairs, rr)
                    T.op("dve", lambda e, po=po, h=h: e.bn_stats(out=st6[b].ap[:, h, :], in_=po.ap), r=[po], w=[st6[b]])
                    T.op("dve", lambda e, h=h: e.bn_aggr(out=mv[b].ap[:, h, :], in_=st6[b].ap[:, h, :]), r=[st6[b]], w=[mv[b]])
                    if upd is not None:
                        state_update_head(upd, h)
                T.op("dve", lambda e: e.tensor_scalar(out=ve[b].ap, in0=mv[b].ap[:, :, 1], scalar1=GN_EPS, scalar2=None, op0=ALU.add), r=[mv[b]], w=[ve[b]])
                T.op("pool", lambda e: e.tensor_tensor(out=rs4[b].ap, in0=ve[b].ap, in1=neghalf.ap[:, 0:4], op=ALU.pow), r=[ve[b], neghalf], w=[rs4[b]])

            def state_update_head(cu, h):
                bu = cu % 2
                for dc in range(2):
                    j = 2 * h + dc
                    pb = P[6 + dc]
                    mm_group(pb, pb.ap, [(Kx[bu].ap[:, j * 128:(j + 1) * 128], vch[bu].ap[:, h * 512:(h + 1) * 512])], [Kx[bu], vch[bu]])
                    T.op("dve", lambda e, pb=pb, j=j, h=h: e.scalar_tensor_tensor(
                        out=S32[j].ap, in0=S32[j].ap, scalar=cdec.ap[:, h:h + 1], in1=pb.ap,
                        op0=ALU.mult, op1=ALU.add), r=[S32[j], cdec, pb], w=[S32[j]])
                    T.op("act", lambda e, j=j: e.activation(out=S16[1 - bu][j].ap, in_=S32[j].ap, func=AF.Copy), r=[S32[j]], w=[S16[1 - bu][j]])

            def stage_k(cu):
                bu = cu % 2
                kdec_transposes(ktc[bu], Kx[bu], 0, P[5])

            def stage_a2(ci):
                b = ci % 2
                T.op("dve", lambda e: e.scalar_tensor_tensor(out=nb4[b].ap, in0=mv[b].ap[:, :, 0], scalar=-1.0, in1=rs4[b].ap, op0=ALU.mult, op1=ALU.mult),
                     r=[mv[b], rs4[b]], w=[nb4[b]])
                for h in range(H):
                    T.op("act", lambda e, h=h: e.activation(out=on[h].ap, in_=P[h].ap, func=AF.Identity,
                                                            scale=rs4[b].ap[:, h:h + 1], bias=nb4[b].ap[:, h:h + 1]),
                         r=[P[h], rs4[b], nb4[b]], w=[on[h]])
                    T.op("dve", lambda e, h=h: e.tensor_tensor(out=t1[h].ap, in0=on[h].ap, in1=sgwc[b].ap[:, h, :], op=ALU.mult),
                         r=[on[h], sgwc[b]], w=[t1[h]])
                    T.op("pool", lambda e, h=h: e.tensor_tensor(out=gated[b][h].ap, in0=t1[h].ap, in1=sgbc[b].ap[:, h, :], op=ALU.add),
                         r=[t1[h], sgbc[b]], w=[gated[b][h]])

            def stage_b(ci):
                b = ci % 2
                for half in range(2):
                    pb = P[5] if half == 0 else P[4]
                    pv = pb.ap.bitcast(BF16)
                    for jj in range(8):
                        f = half * 8 + jj
                        T.op("pe", lambda e, pv=pv, jj=jj, f=f: e.transpose(out=pv[:, jj * 128:(jj + 1) * 128],
                                                                            in_=gated[b][f // 4].ap[:, (f % 4) * 128:(f % 4 + 1) * 128],
                                                                            identity=ident.ap), r=[gated[b][f // 4], ident], w=[pb], inc=(jj == 7))
                    T.op("act", lambda e, pv=pv, half=half: e.activation(out=gTt[b].ap[:, half * 8:half * 8 + 8, :], in_=pv.rearrange("p (j n) -> p j n", j=8), func=AF.Copy),
                         r=[pb], w=[gTt[b]])
                T.op("act", lambda e: e.dma_start(out=sc["gT"][ci].rearrange("p (j n) -> p j n", j=16), in_=gTt[b].ap),
                     r=[gTt[b]], w=[dkey("gT", i, ci)], dma=True)

            loads_a(0)
            loads_g(0)
            if nch > 1:
                loads_a(1)
            stage_q(0)
            for n in range(nch + 1):
                if n < nch:
                    stage_s(n)
                    if n + 1 < nch:
                        stage_k(n)
                if n >= 1:
                    stage_a2(n - 1)
                    if n < nch:
                        loads_g(n)
                if n < nch:
                    stage_o(n, n if n + 1 < nch else None)
                    if n + 2 < nch:
                        loads_a(n + 2)
                if n >= 1:
                    stage_b(n - 1)
                if n + 1 < nch:
                    stage_q(n + 1)

        def sweep3a(l, i, xsrc, xsrc_name, xdst, xdst_name):
            S = seq_lens[i]
            sc = scr[i]
            A.reset(base_off)
            ring = Ring(6, 3)
            zts = [A.alloc([8, TS + 2], F32) for _ in range(2)]
            cbts = [A.alloc([8, TS], BF16) for _ in range(2)]
            uTs = [A.alloc([8, TS], BF16) for _ in range(2)]
            gat = A.alloc([16, TS], BF16)
            gTt = A.alloc([16, TS], BF16)
            xt = A.alloc([4, D], F32)
            ca = [A.alloc([TS], F32) for _ in range(2)]
            cb_ = [A.alloc([TS], F32) for _ in range(2)]
            ta = [A.alloc([TS], F32) for _ in range(2)]
            tb = [A.alloc([TS], F32) for _ in range(2)]
            mT = A.alloc([8, TS], BF16)
            ntile = S // TS
            per = [22, 24, 26, 23, 25, 27, 28, 29]
            plan = []
            for t in range(ntile):
                plan += [l * NU_L + u for u in per]
            ring.set_plan(plan)

            def loads_conv(t):
                t0 = t * TS
                zt, cbt = zts[t % 2], cbts[t % 2]
                zkeys = [dkey("zT", i, t, 0), dkey("zT", i, t, 1)]
                if t > 0:
                    zkeys += [dkey("zT", i, t - 1, 0), dkey("zT", i, t - 1, 1)]
                else:
                    zkeys.append(dkey("zTh", i, 0))
                if t + 1 < ntile:
                    zkeys += [dkey("zT", i, t + 1, 0), dkey("zT", i, t + 1, 1)]
                else:
                    zkeys.append(dkey("zTh", i, S + 1))
                T.op("sp", lambda e: e.dma_start(out=zt.ap, in_=sc["zT"].rearrange("j p s -> p j s")[:, :, t0:t0 + TS + 2]),
                     r=zkeys, w=[zt], dma=True)
                T.op("sp", lambda e: e.dma_start(out=cbt.ap, in_=sc["cbT"].rearrange("j p s -> p j s")[:, :, t0:t0 + TS]),
                     r=[dkey("cbT", i, t, 0), dkey("cbT", i, t, 1)], w=[cbt], dma=True)

            def loads_g(t):
                t0 = t * TS
                T.op("sp", lambda e: e.dma_start(out=gat.ap, in_=sc["gaT"].rearrange("j p s -> p j s")[:, :, t0:t0 + TS]),
                     r=[dkey("gaT", i, t, uu) for uu in range(4)], w=[gat], dma=True)
                for c in range(4):
                    ci = t * 4 + c
                    T.op("sp", lambda e, ci=ci, c=c: e.dma_start(out=gTt.ap[:, :, c * 128:(c + 1) * 128],
                                                                 in_=sc["gT"][ci].rearrange("p (j n) -> p j n", j=16)),
                         r=[dkey("gT", i, ci)], w=[gTt], dma=True)

            def loads_x(t):
                t0 = t * TS
                T.op("sp", lambda e: e.dma_start(out=xt.ap, in_=xsrc[t0:t0 + TS, :].rearrange("(c p) d -> p c d", p=128)),
                     r=[dkey(xsrc_name, i, t * 4 + c) for c in range(4)], w=[xt], dma=True)

            def conv(t):
                zt, cbt, uT = zts[t % 2], cbts[t % 2], uTs[t % 2]
                for j in range(8):
                    a_, b_ = ca[j % 2], cb_[j % 2]
                    T.op("act", lambda e, j=j, a_=a_: e.activation(out=a_.ap, in_=zt.ap[:, j, 1:TS + 1], func=AF.Copy, scale=cw.ap[:, 1, j:j + 1]),
                         r=[zt, cw], w=[a_])
                    T.op("dve", lambda e, j=j, a_=a_, b_=b_: e.scalar_tensor_tensor(out=b_.ap, in0=zt.ap[:, j, 0:TS], scalar=cw.ap[:, 0, j:j + 1], in1=a_.ap,
                                                                                  op0=ALU.mult, op1=ALU.add), r=[zt, cw, a_], w=[b_])
                    T.op("dve", lambda e, j=j, a_=a_, b_=b_: e.scalar_tensor_tensor(out=a_.ap, in0=zt.ap[:, j, 2:TS + 2], scalar=cw.ap[:, 2, j:j + 1], in1=b_.ap,
                                                                                  op0=ALU.mult, op1=ALU.add), r=[zt, cw, b_], w=[a_])
                    T.op("pool", lambda e, j=j, a_=a_: e.tensor_tensor(out=uT.ap[:, j, :], in0=a_.ap, in1=cbt.ap[:, j, :], op=ALU.mult),
                         r=[a_, cbt], w=[uT])

            loads_conv(0)
            loads_g(0)
            loads_x(0)
            conv(0)
            for t in range(ntile):
                t0 = t * TS
                base = t * 8
                uT = uTs[t % 2]
                ring.prefetch()
                if t + 1 < ntile:
                    loads_conv(t + 1)
                for j in range(8):
                    ch, jj = j // 4, j % 4
                    wco = ring.get(base + ch * 3 + 0)
                    pb = bank()
                    mm_group(pb, pb.ap, [(wco.ap[:, k, jj * 128:(jj + 1) * 128], uT.ap[:, k, :]) for k in range(KD)], [wco, uT])
                    ta_, tb_ = ta[j % 2], tb[j % 2]
                    T.op("dve", lambda e, pb=pb, j=j, ta_=ta_: e.tensor_tensor(out=ta_.ap, in0=pb.ap, in1=gat.ap[:, j, :], op=ALU.mult), r=[pb, gat], w=[ta_])
                    wr0 = ring.get(base + ch * 3 + 1)
                    wr1 = ring.get(base + ch * 3 + 2)
                    pb2 = bank()
                    pairs = [((wr0 if k < 8 else wr1).ap[:, k % 8, jj * 128:(jj + 1) * 128], gTt.ap[:, k, :]) for k in range(16)]
                    mm_group(pb2, pb2.ap, pairs, [wr0, wr1, gTt])
                    T.op("dve", lambda e, pb2=pb2, j=j, tb_=tb_: e.tensor_tensor(out=tb_.ap, in0=pb2.ap, in1=gat.ap[:, 8 + j, :], op=ALU.mult), r=[pb2, gat], w=[tb_])
                    T.op("pool", lambda e, j=j, ta_=ta_, tb_=tb_: e.tensor_tensor(out=mT.ap[:, j, :], in0=ta_.ap, in1=tb_.ap, op=ALU.add), r=[ta_, tb_], w=[mT])
                if t + 1 < ntile:
                    loads_g(t + 1)
                    conv(t + 1)
                for ch in range(2):
                    wm = ring.get(base + 6 + ch)
                    for c in range(4):
                        pb = bank()
                        mm_group(pb, pb.ap, [(mT.ap[:, k, c * 128:(c + 1) * 128], wm.ap[:, k, :]) for k in range(KD)], [wm, mT])
                        T.op("dve", lambda e, pb=pb, c=c, ch=ch: e.tensor_tensor(out=xt.ap[:, c, ch * 512:(ch + 1) * 512], in0=pb.ap,
                                                                                 in1=xt.ap[:, c, ch * 512:(ch + 1) * 512], op=ALU.add), r=[pb, xt], w=[xt])
                T.op("act", lambda e, t0=t0: e.dma_start(out=xdst[t0:t0 + TS, :].rearrange("(c p) d -> p c d", p=128), in_=xt.ap),
                     r=[xt], w=[dkey(xdst_name, i, t * 4 + c) for c in range(4)], dma=True)
                if t + 1 < ntile:
                    loads_x(t + 1)

        def sweep3b(l, i, xsrc, xsrc_name, xdst, xdst_name, final):
            S = seq_lens[i]
            A.reset(base_off)
            ring = Ring(6, 1)
            xt = [A.alloc([4, D], F32) for _ in range(2)]
            xs = [A.alloc([D], BF16) for _ in range(4)]
            junk = A.alloc([D], BF16)
            junkf = A.alloc([D], F32)
            ss = [A.alloc([1], F32) for _ in range(6)]
            rstd = [A.alloc([1], F32) for _ in range(6)]
            h2Ts = [A.alloc([KD, TS], BF16) for _ in range(2)]
            gtab = A.alloc([D], F32)
            rl = [A.alloc([TS], F32) for _ in range(4)]
            hid = A.alloc([32, TS], BF16)
            nfin = A.alloc([D], F32)
            T.op("sp", lambda e: e.dma_start(out=gtab.ap, in_=norm_mlp[l:l + 1, :].partition_broadcast(128)[:, 0, :]), w=[gtab], dma=True)
            if final:
                T.op("sp", lambda e: e.dma_start(out=nfin.ap, in_=norm_final[0:1, :].partition_broadcast(128)[:, 0, :]), w=[nfin], dma=True)
            ntile = S // TS
            per = list(range(30, 38)) + [38, 40, 42, 44, 39, 41, 43, 45]
            plan = []
            for t in range(ntile):
                plan += [l * NU_L + u for u in per]
            ring.set_plan(plan)
            cnt = [0]

            def load(t):
                t0 = t * TS
                T.op("sp", lambda e: e.dma_start(out=xt[t % 2].ap, in_=xsrc[t0:t0 + TS, :].rearrange("(c p) d -> p c d", p=128)),
                     r=[dkey(xsrc_name, i, t * 4 + c) for c in range(4)], w=[xt[t % 2]], dma=True)

            def front_a(t):
                X = xt[t % 2]
                for c in range(4):
                    norm_a(X.ap[:, c, :], X, gtab, xs[c], ss[c], rstd[c], junk)

            def front_b(t):
                for c in range(4):
                    norm_b(xs[c], h2Ts[t % 2], c)

            load(0)
            front_a(0)
            front_b(0)
            for t in range(ntile):
                t0 = t * TS
                base = t * 16
                X = xt[t % 2]
                h2T = h2Ts[t % 2]
                ring.prefetch()
                for u in range(8):
                    wslot = ring.get(base + u)
                    for jj in range(4):
                        f = u * 4 + jj
                        pb = bank()
                        mm_group(pb, pb.ap, [(wslot.ap[:, k, jj * 128:(jj + 1) * 128], h2T.ap[:, k, :]) for k in range(KD)], [wslot, h2T])
                        r_ = rl[f % 4]
                        T.op("act", lambda e, pb=pb, r_=r_: e.activation(out=r_.ap, in_=pb.ap, func=AF.Relu), r=[pb], w=[r_])
                        T.op("dve", lambda e, r_=r_, f=f: e.tensor_tensor(out=hid.ap[:, f, :], in0=r_.ap, in1=r_.ap, op=ALU.mult),
                             r=[r_], w=[hid])
                if t + 1 < ntile:
                    load(t + 1)
                    front_a(t + 1)
                for ch in range(2):
                    pbs = [bank() for _ in range(4)]
                    for kq in range(4):
                        wslot = ring.get(base + 8 + ch * 4 + kq)
                        for c in range(4):
                            for k in range(8):
                                first = (kq == 0 and k == 0)
                                last = (kq == 3 and k == 7)
                                T.op("pe", lambda e, c=c, k=k, kq=kq, wslot=wslot, first=first, last=last, pbs=pbs: e.matmul(
                                    pbs[c].ap, lhsT=hid.ap[:, kq * 8 + k, c * 128:(c + 1) * 128], rhs=wslot.ap[:, k, :], start=first, stop=last),
                                    r=[hid, wslot], w=[pbs[c]], inc=(k == 7))
                    if ch == 0 and t + 1 < ntile:
                        front_b(t + 1)
                    for c in range(4):
                        T.op("dve", lambda e, c=c, ch=ch, pbs=pbs, X=X: e.tensor_tensor(out=X.ap[:, c, ch * 512:(ch + 1) * 512], in0=pbs[c].ap,
                                                                                        in1=X.ap[:, c, ch * 512:(ch + 1) * 512], op=ALU.add), r=[pbs[c], X], w=[X])
                if final:
                    for c in range(4):
                        b = 4 + c % 2
                        T.op("act", lambda e, c=c, b=b, X=X: e.activation(out=xs[c].ap, in_=X.ap[:, c, :], func=AF.Square, accum_out=ss[b].ap[:, 0:1]),
                             r=[X], w=[xs[c], ss[b]])
                        rstd_from_ss(ss[b], rstd[b], 1, 1.0 / D, NORM_EPS)
                        T.op("dve", lambda e, c=c, b=b, X=X: e.scalar_tensor_tensor(out=X.ap[:, c, :], in0=X.ap[:, c, :], scalar=rstd[b].ap[:, 0:1], in1=nfin.ap,
                                                                                  op0=ALU.mult, op1=ALU.mult), r=[X, rstd[b], nfin], w=[X])
                T.op("act", lambda e, t0=t0, X=X: e.dma_start(out=xdst[t0:t0 + TS, :].rearrange("(c p) d -> p c d", p=128), in_=X.ap),
                     r=[X], w=[dkey(xdst_name, i, t * 4 + c) for c in range(4)], dma=True)
        import os as _os
        _stop = int(_os.environ.get("MK_STOP", "99"))
        for l in range(depth):
            if _stop < 1:
                break
            layer_consts(l)
            for i in range(nseq):
                sc = scr[i]
                if l == 0:
                    xsrc, xname = xin[i], "xin"
                else:
                    xsrc, xname = sc["xb"], "xb%d" % (l - 1)
                if _stop >= 2:
                    sweep1(l, i, xsrc, xname)
                if _stop >= 3:
                    sweep2a(l, i)
                if _stop >= 4:
                    sweep2b(l, i)
                if _stop >= 5:
                    sweep3a(l, i, xsrc, xname, sc["xa"], "xa%d" % l)
                final = (l == depth - 1)
                if _stop < 6:
                    continue
                if final:
                    sweep3b(l, i, sc["xa"], "xa%d" % l, yout[i], "yout", True)
                else:
                    sweep3b(l, i, sc["xa"], "xa%d" % l, sc["xb"], "xb%d" % l, False)
        T.finish()

        with nc.Block() as block:
            @block.tensor
            def _(e):
                T.replay("pe", e)

            @block.scalar
            def _(e):
                T.replay("act", e)

            @block.vector
            def _(e):
                T.replay("dve", e)

            @block.gpsimd
            def _(e):
                T.replay("pool", e)

            @block.sync
            def _(e):
                T.replay("sp", e)
    return nc


def _tables(smax):
    pos = np.arange(smax, dtype=np.float32)
    theta = (1.0 / (10000.0 ** np.linspace(0.0, 1.0, DK // 2, dtype=np.float32))).astype(np.float32)
    ang = (pos[None, :] * theta[:, None]).astype(np.float32)
    cos = np.cos(ang.astype(np.float64)).astype(np.float32)
    sin = np.sin(ang.astype(np.float64)).astype(np.float32)
    n = np.arange(128, dtype=np.float32)
    diff = n[None, :] - n[:, None]
    cst = np.zeros((128, 6 * 128 + 2), np.float32)
    cst[:, 0:128] = np.maximum(diff, 0)
    cst[:, 128:256] = np.maximum(-diff, 0)
    cst[:, 256:384] = (diff >= 0) * 0.0625
    cst[:, 384:512] = (diff < 0) * 0.0625
    cst[:, 512:640] = n[None, :] + 1.0
    cst[:, 640:768] = 128.0 - n[None, :]
    cst[:, 768] = n
    cst[:, 769] = 127.0 - n
    return np.ascontiguousarray(cos), np.ascontiguousarray(sin), cst


_CACHE = {}


def run(seq_arrays_per_core, params, depth):
    seq_lens = tuple(a.shape[0] for a in seq_arrays_per_core[0])
    smax = max(seq_lens)
    key = (seq_lens, depth)
    if key not in _CACHE:
        _CACHE[key] = build_program(list(seq_lens), depth, smax)
    nc = _CACHE[key]
    cos, sin, cst = _tables(smax)
    shared = {k: np.ascontiguousarray(np.asarray(v, dtype=np.float32)) for k, v in params.items()}
    shared["norm_final"] = shared["norm_final"].reshape(1, D)
    shared["costab"] = cos
    shared["sintab"] = sin
    shared["consts"] = cst
    in_maps = []
    for seqs in seq_arrays_per_core:
        m = dict(shared)
        for i, a in enumerate(seqs):
            m["x%d" % i] = np.ascontiguousarray(a, dtype=np.float32)
        in_maps.append(m)
    ncores = len(seq_arrays_per_core)
    import os as _os
    if _os.environ.get("MK_TRACE"):
        res = run_bass_kernel_spmd(nc, in_maps, core_ids=list(range(ncores)), trace=True)
        print("EXEC_TIME_NS", res.exec_time_ns, flush=True)
    else:
        res = run_bass_kernel_spmd(nc, in_maps, core_ids=list(range(ncores)))
    return [[r["y%d" % i] for i in range(len(seq_lens))] for r in res.results]


def kernel(x_prompt, x_sample, norm_mix, w_in, conv_w, w_conv_out, ret_decay_fwd, ret_decay_bwd,
           ret_gn_w, ret_gn_b, w_ret_out, gate_b, w_mix_out, norm_mlp, w_mlp_in, w_mlp_out, norm_final):
    x_prompt = np.asarray(x_prompt, dtype=np.float32)
    x_sample = np.asarray(x_sample, dtype=np.float32)
    params = dict(norm_mix=norm_mix, w_in=w_in, conv_w=conv_w, w_conv_out=w_conv_out, ret_decay_fwd=ret_decay_fwd,
                  ret_decay_bwd=ret_decay_bwd, ret_gn_w=ret_gn_w, ret_gn_b=ret_gn_b, w_ret_out=w_ret_out, gate_b=gate_b,
                  w_mix_out=w_mix_out, norm_mlp=norm_mlp, w_mlp_in=w_mlp_in, w_mlp_out=w_mlp_out, norm_final=norm_final)
    depth = np.asarray(w_in).shape[0]
    ncores = 8
    per_core = [[x_prompt[0], x_sample[c]] for c in range(ncores)]
    outs = run(per_core, params, depth)
    sp = x_prompt.shape[1]
    piece = sp // ncores
    y_prompt = np.concatenate([outs[c][0][c * piece:(c + 1) * piece] for c in range(ncores)], axis=0)[None]
    y_sample = np.stack([outs[c][1] for c in range(ncores)], axis=0)
    return (y_prompt.astype(np.float32), y_sample.astype(np.float32))
```

```python
import math
from contextlib import ExitStack
import numpy as np
import concourse.bass as bass
import concourse.mybir as mybir
from concourse.bass_utils import run_bass_kernel_spmd

F32 = mybir.dt.float32
BF16 = mybir.dt.bfloat16
ALU = mybir.AluOpType
AF = mybir.ActivationFunctionType

D = 1024
KD = 8
H = 4
DK = 256
DV = 512
RV = 2048
DFF = 4096
INC = 11264
C = 128
TS = 512
NORM_EPS = 1e-6
GN_EPS = 1e-5
NU_L = 46
ARENA = 211968
BLK = 512
SEM_LIMIT = 28000
NDSEM = 16


class Buf:
    def __init__(self, ap, keys):
        self.ap = ap
        self.keys = keys

    def __getitem__(self, idx):
        return self.ap[idx]


class Stream:
    def __init__(self, name):
        self.name = name
        self.items = []
        self.sem = None
        self.cnt = 0
        self.waited = {}
        self.dsems = []
        self.dn = 0


class Tracker:
    def __init__(self, nc, es):
        self.nc = nc
        self.es = es
        self.streams = {n: Stream(n) for n in ("pe", "act", "dve", "pool", "sp")}
        self.state = {}
        self.nsem = 0
        for s in self.streams.values():
            s.sem = self._newsem(s.name)
        for n in ("sp", "pool", "act"):
            self.streams[n].dsems = [self._newsem(n + "d%d" % i) for i in range(NDSEM)]
        self.all_dma_events = []

    def _newsem(self, name):
        self.nsem += 1
        return self.es.enter_context(self.nc.semaphore("%s_%d" % (name, self.nsem)))

    def _wait(self, st, ev):
        sem, val = ev
        k = id(sem)
        if st.waited.get(k, 0) >= val:
            return
        st.waited[k] = val
        st.items.append(("w", sem, val))

    def op(self, stream, fn, r=(), w=(), dma=False, inc=True):
        st = self.streams[stream]
        compute = not dma
        deps = []
        for b in r:
            for k in b.keys:
                s = self.state.get(k)
                if s is not None and s[0] is not None:
                    deps.append(s[0])
        for b in w:
            for k in b.keys:
                s = self.state.get(k)
                if s is not None:
                    if s[0] is not None:
                        deps.append((s[0][0], s[0][1], "w"))
                    deps.extend((x[0], x[1], "r") for x in s[1])
        if dma:
            slot = st.dn % NDSEM
            rnd = st.dn // NDSEM
            st.dn += 1
            dsem = st.dsems[slot]
            if rnd > 0:
                deps.append((dsem, 16 * rnd))
            ev = (dsem, 16 * (rnd + 1))
            self.all_dma_events.append(ev)
        else:
            if st.cnt + 1 > SEM_LIMIT and not getattr(st, "in_group", False):
                st.sem = self._newsem(st.name)
                st.cnt = 0
            st.in_group = not inc
            ev = (st.sem, st.cnt + 1)
            if inc:
                st.cnt += 1
        for d in deps:
            sem, val = d[0], d[1]
            if compute and sem is st.sem:
                if stream == "pe" or len(d) == 3:
                    continue
                if val > st.cnt - (1 if inc else 0):
                    continue
            self._wait(st, (sem, val))
        if dma:
            st.items.append(("d", fn, ev[0]))
        else:
            st.items.append(("c", fn, ev[0] if inc else None))
        for b in r:
            for k in b.keys:
                s = self.state.get(k)
                if s is None:
                    s = [None, []]
                    self.state[k] = s
                s[1].append(ev)
        for b in w:
            for k in b.keys:
                self.state[k] = [ev, []]
        return ev

    def finish(self):
        st = self.streams["sp"]
        last = {}
        for ev in self.all_dma_events:
            last[id(ev[0])] = ev
        for ev in last.values():
            self._wait(st, ev)
        for n, s in self.streams.items():
            if n != "sp" and s.cnt > 0:
                self._wait(st, (s.sem, s.cnt))

    def replay(self, name, eng):
        for it in self.streams[name].items:
            if it[0] == "w":
                eng.wait_ge(it[1], it[2])
            elif it[0] == "d":
                it[1](eng).then_inc(it[2], 16)
            else:
                ins = it[1](eng)
                if it[2] is not None:
                    ins.then_inc(it[2], 1)


class Arena:
    def __init__(self, tensor):
        self.t = tensor
        self.off = 0

    def reset(self, off=0):
        self.off = off

    def alloc(self, shape, dtype):
        es = 2 if dtype == BF16 else 4
        n = int(np.prod(shape))
        nbytes = n * es
        start = (self.off + BLK - 1) // BLK * BLK
        end = start + nbytes
        assert end <= ARENA, ("arena overflow", end)
        self.off = end
        ap = self.t[:, start // 2:(start + nbytes) // 2]
        if dtype != BF16:
            ap = ap.bitcast(dtype)
        if len(shape) == 2:
            ap = ap.rearrange("p (a b) -> p a b", a=shape[0])
        elif len(shape) == 3:
            ap = ap.rearrange("p (a b c) -> p a b c", a=shape[0], b=shape[1])
        keys = [("sb", b) for b in range(start // BLK, (end + BLK - 1) // BLK)]
        return Buf(ap, keys)


def build_program(seq_lens, depth, smax, rlite=True):
    nc = bass.Bass("TRN2", target_bir_lowering=False)
    nseq = len(seq_lens)
    rlite = bool(rlite and depth >= 2 and seq_lens[0] % (8 * TS) == 0)
    OWN = seq_lens[0] // 8
    OT = OWN // TS
    SL = (OT + 2) * TS
    NLC = SL // C
    seq_lens = list(seq_lens) + ([SL] if rlite else [])
    I32 = mybir.dt.int32
    U32 = mybir.dt.uint32

    def din(name, shape, dt=F32):
        return nc.dram_tensor(name, list(shape), dt, kind="ExternalInput").ap()

    def dscr(name, shape, dt):
        return nc.dram_tensor(name, list(shape), dt, kind="Internal").ap()

    xin = [din("x%d" % i, [s, D]) for i, s in enumerate(seq_lens[:nseq])]
    yout = [nc.dram_tensor("y%d" % i, [(OWN if (rlite and i == 0) else s), D], F32, kind="ExternalOutput").ap()
            for i, s in enumerate(seq_lens[:nseq])]
    if rlite:
        oidx_x = nc.dram_tensor("oidx_x", [NLC, 128, 1], I32, kind="ExternalInput").ap()
        oidx_c = nc.dram_tensor("oidx_c", [NLC, 128, 1], I32, kind="ExternalInput").ap()
        cos_own = din("cos_own", [128, SL])
        sin_own = din("sin_own", [128, SL])
    w_in = din("w_in", [depth, D, INC])
    w_co = din("w_conv_out", [depth, D, D])
    w_ro = din("w_ret_out", [depth, RV, D])
    w_mx = din("w_mix_out", [depth, D, D])
    w_mi = din("w_mlp_in", [depth, D, DFF])
    w_mo = din("w_mlp_out", [depth, DFF, D])
    norm_mix = din("norm_mix", [depth, D])
    norm_mlp = din("norm_mlp", [depth, D])
    norm_final = din("norm_final", [1, D])
    conv_w = din("conv_w", [depth, 3, D])
    dec_f = din("ret_decay_fwd", [depth, H])
    dec_b = din("ret_decay_bwd", [depth, H])
    gn_w = din("ret_gn_w", [depth, RV])
    gn_b = din("ret_gn_b", [depth, RV])
    gate_b = din("gate_b", [depth, 2 * D])
    costab = din("costab", [128, smax])
    sintab = din("sintab", [128, smax])
    consts = din("consts", [128, 6 * 128 + 2])

    wq = dscr("wq", [depth * NU_L, 128, 4096], BF16)
    scr = []
    for i, s in enumerate(seq_lens):
        nch = s // C
        scr.append(dict(
            zT=dscr("zT%d" % i, [8, 128, s + 2], F32),
            cbT=dscr("cbT%d" % i, [8, 128, s], BF16),
            gaT=dscr("gaT%d" % i, [16, 128, s], BF16),
            qT=dscr("qT%d" % i, [nch, 128, 8 * 128], BF16),
            kT=dscr("kT%d" % i, [nch, 128, 8 * 128], BF16),
            v=dscr("v%d" % i, [s, RV], BF16),
            sgw=dscr("sgw%d" % i, [s, RV], BF16),
            sgb=dscr("sgb%d" % i, [s, RV], BF16),
            sb=dscr("sb%d" % i, [nch, 128, 8 * 512], BF16),
            gT=dscr("gT%d" % i, [nch, 128, 16 * 128], BF16),
            xa=dscr("xa%d" % i, [s, D], F32),
            xb_pad=dscr("xb%d" % i, [s + 2 * TS, D], F32),
            sf=dscr("sf%d" % i, [nch, 128, 8 * 512], BF16),
        ))
        scr[-1]["xb"] = scr[-1]["xb_pad"][TS:TS + s, :]

    es = ExitStack()
    with es:
        arena_t = es.enter_context(nc.sbuf_tensor("arena", [128, ARENA // 2], BF16))
        A = Arena(arena_t)
        psb = [es.enter_context(nc.psum_tensor("ps%d" % i, [128, 512], F32)) for i in range(8)]
        T = Tracker(nc, es)
        ps_bufs = [Buf(psb[i][:, :], [("ps", i)]) for i in range(8)]
        ps_ctr = [0]

        def bank():
            b = ps_bufs[ps_ctr[0] % 8]
            ps_ctr[0] += 1
            return b

        def dkey(name, *idx):
            return Buf(None, [("dram", name) + tuple(idx)])

        ident_f = A.alloc([128], F32)
        ident = A.alloc([128], BF16)
        cst = A.alloc([6 * 128 + 2], F32)
        neghalf = A.alloc([8], F32)
        gmix = A.alloc([KD], F32)
        gmlp = A.alloc([KD], F32)
        cw = A.alloc([3, 8], F32)
        gbias = A.alloc([16], F32)
        dcy = A.alloc([8], F32)
        lg = A.alloc([8], F32)
        kdec = A.alloc([8], F32)
        cdec = A.alloc([8], F32)
        base_off = A.off

        cA = lambda: cst[:, 0:128]
        cB = lambda: cst[:, 128:256]
        cMf = lambda: cst[:, 256:384]
        cMb = lambda: cst[:, 384:512]
        cNp1 = lambda: cst[:, 512:640]
        cCmn = lambda: cst[:, 640:768]
        cP = lambda: cst[:, 768:769]
        cRp = lambda: cst[:, 769:770]

        T.op("sp", lambda e: e.dma_start(out=cst.ap, in_=consts[:, :]), w=[cst], dma=True)
        T.op("pool", lambda e: e.memset(neghalf.ap, -0.5), w=[neghalf])
        T.op("pool", lambda e: e.memset(ident_f.ap, 1.0), w=[ident_f])
        T.op("pool", lambda e: e.affine_select(out=ident_f.ap, in_=ident_f.ap, pattern=[[-1, 128]],
                                               compare_op=ALU.is_equal, fill=0.0, base=0, channel_multiplier=1),
             r=[ident_f], w=[ident_f])
        T.op("dve", lambda e: e.tensor_copy(out=ident.ap, in_=ident_f.ap), r=[ident_f], w=[ident])

        def unit_src(l, u):
            if u < 22:
                return w_in[l].rearrange("(k p) c -> p k c", p=128)[:, :, u * 512:(u + 1) * 512]
            if u < 24:
                return w_co[l].rearrange("(k p) c -> p k c", p=128)[:, :, (u - 22) * 512:(u - 21) * 512]
            if u < 28:
                kh, ch = (u - 24) // 2, (u - 24) % 2
                return w_ro[l].rearrange("(k p) c -> p k c", p=128)[:, kh * 8:(kh + 1) * 8, ch * 512:(ch + 1) * 512]
            if u < 30:
                return w_mx[l].rearrange("(k p) c -> p k c", p=128)[:, :, (u - 28) * 512:(u - 27) * 512]
            if u < 38:
                return w_mi[l].rearrange("(k p) c -> p k c", p=128)[:, :, (u - 30) * 512:(u - 29) * 512]
            kq, ch = (u - 38) // 2, (u - 38) % 2
            return w_mo[l].rearrange("(k p) c -> p k c", p=128)[:, kq * 8:(kq + 1) * 8, ch * 512:(ch + 1) * 512]

        for l in range(depth):
            for u in range(NU_L):
                dst = wq[l * NU_L + u].rearrange("p (k c) -> p k c", k=8)
                src = unit_src(l, u)
                T.op("pool", lambda e, dst=dst, src=src: e.dma_start(out=dst, in_=src),
                     w=[dkey("wq", l * NU_L + u)], dma=True)

        zcol = A.alloc([8, 1], F32)
        base_off = A.off
        T.op("pool", lambda e: e.memset(zcol.ap, 0.0), w=[zcol])
        for i, s in enumerate(seq_lens):
            for col in (0, s + 1):
                T.op("sp", lambda e, i=i, col=col: e.dma_start(
                    out=scr[i]["zT"].rearrange("j p s -> p j s")[:, :, col:col + 1], in_=zcol.ap, allow_slow_non_contiguous=True),
                    r=[zcol], w=[dkey("zTh", i, col)], dma=True)

        if rlite:
            zpad = A.alloc([4, D], F32)
            T.op("pool", lambda e: e.memset(zpad.ap, 0.0), w=[zpad])
            for r0 in (0, TS + seq_lens[0]):
                T.op("sp", lambda e, r0=r0: e.dma_start(out=scr[0]["xb_pad"][r0:r0 + TS, :].rearrange("(c p) d -> p c d", p=128), in_=zpad.ap),
                     r=[zpad], w=[dkey("xbpad", r0)], dma=True)
            A.reset(base_off)

        def dma_barrier(stream):
            last = {}
            for ev in T.all_dma_events:
                last[id(ev[0])] = ev
            for ev in last.values():
                T._wait(T.streams[stream], ev)

        def gather_rows(dst_ap, dst_buf, src2d, idx_src, idxbuf):
            T.op("sp", lambda e: e.dma_start(out=idxbuf.ap, in_=idx_src), w=[idxbuf], dma=True)
            T.op("pool", lambda e: e.indirect_dma_start(out=dst_ap, out_offset=None, in_=src2d,
                                                        in_offset=bass.IndirectOffsetOnAxis(ap=idxbuf.ap.bitcast(U32), axis=0)),
                 r=[idxbuf], w=[dst_buf], dma=True)

        class Ring:
            def __init__(self, nslots, hold):
                self.slots = [A.alloc([8, 512], BF16) for _ in range(nslots)]
                self.hold = hold
                self.cur = {}
                self.n = 0
                self.plan = []
                self.loaded = 0
                self.used = 0

            def set_plan(self, units):
                self.plan = list(units)
                self.loaded = 0
                self.used = 0

            def _load_one(self):
                u = self.plan[self.loaded]
                slot = self.slots[self.n % len(self.slots)]
                self.n += 1
                self.loaded += 1
                T.op("sp", lambda e, slot=slot, u=u: e.dma_start(
                    out=slot.ap, in_=wq[u].rearrange("p (k c) -> p k c", k=8)),
                    r=[dkey("wq", u)], w=[slot], dma=True)
                return slot

            def prefetch(self):
                ahead = len(self.slots) - self.hold + 1
                while self.loaded < len(self.plan) and self.loaded < self.used + ahead:
                    idx = self.loaded
                    self.cur[idx] = self._load_one()

            def get(self, idx):
                self.used = max(self.used, idx)
                self.prefetch()
                return self.cur[idx]

        def rstd_from_ss(ss, rstd, n, scale, eps):
            T.op("dve", lambda e: e.tensor_scalar(out=ss.ap[:, 0:n], in0=ss.ap[:, 0:n], scalar1=scale, scalar2=eps,
                                                  op0=ALU.mult, op1=ALU.add), r=[ss], w=[ss])
            T.op("pool", lambda e: e.tensor_tensor(out=rstd.ap[:, 0:n], in0=ss.ap[:, 0:n], in1=neghalf.ap[:, 0:n],
                                                   op=ALU.pow), r=[ss, neghalf], w=[rstd])

        def layer_consts(l):
            T.op("sp", lambda e: e.dma_start(out=cw.ap, in_=conv_w[l].rearrange("t (j p) -> p t j", p=128),
                                             allow_slow_non_contiguous=True), w=[cw], dma=True)
            T.op("sp", lambda e: e.dma_start(out=gbias.ap, in_=gate_b[l].rearrange("(j p) -> p j", p=128),
                                             allow_slow_non_contiguous=True), w=[gbias], dma=True)
            T.op("sp", lambda e: e.dma_start(out=dcy.ap[:, 0:4], in_=dec_f[l:l + 1, :].partition_broadcast(128)[:, 0, :]),
                 w=[dcy], dma=True)
            T.op("sp", lambda e: e.dma_start(out=dcy.ap[:, 4:8], in_=dec_b[l:l + 1, :].partition_broadcast(128)[:, 0, :]),
                 w=[dcy], dma=True)
            T.op("act", lambda e: e.activation(out=lg.ap, in_=dcy.ap, func=AF.Exp), r=[dcy], w=[lg])
            T.op("dve", lambda e: e.tensor_scalar(out=lg.ap, in0=lg.ap, scalar1=-1.0, scalar2=1.0, op0=ALU.mult, op1=ALU.add), r=[lg], w=[lg])
            T.op("act", lambda e: e.activation(out=lg.ap, in_=lg.ap, func=AF.Ln), r=[lg], w=[lg])
            for h in range(H):
                T.op("act", lambda e, h=h: e.activation(out=kdec.ap[:, h:h + 1], in_=cRp(), func=AF.Exp, scale=lg.ap[:, h:h + 1]),
                     r=[cst, lg], w=[kdec])
                T.op("act", lambda e, h=h: e.activation(out=kdec.ap[:, 4 + h:5 + h], in_=cP(), func=AF.Exp, scale=lg.ap[:, 4 + h:5 + h]),
                     r=[cst, lg], w=[kdec])
            T.op("dve", lambda e: e.tensor_scalar(out=kdec.ap, in0=kdec.ap, scalar1=0.0625, scalar2=None, op0=ALU.mult),
                 r=[kdec], w=[kdec])
            T.op("act", lambda e: e.activation(out=cdec.ap, in_=lg.ap, func=AF.Exp, scale=float(C)), r=[lg], w=[cdec])


        def ret_consts(l, DT, qdF, qdB, tmpc):
            for h in range(H):
                T.op("act", lambda e, h=h: e.activation(out=tmpc.ap[:, 0, :], in_=cA(), func=AF.Exp, scale=lg.ap[:, h:h + 1]),
                     r=[cst, lg], w=[tmpc])
                T.op("act", lambda e, h=h: e.activation(out=tmpc.ap[:, 1, :], in_=cB(), func=AF.Exp, scale=lg.ap[:, 4 + h:5 + h]),
                     r=[cst, lg], w=[tmpc])
                T.op("dve", lambda e: e.tensor_tensor(out=tmpc.ap[:, 0, :], in0=tmpc.ap[:, 0, :], in1=cMf(), op=ALU.mult),
                     r=[tmpc, cst], w=[tmpc])
                T.op("dve", lambda e: e.tensor_tensor(out=tmpc.ap[:, 1, :], in0=tmpc.ap[:, 1, :], in1=cMb(), op=ALU.mult),
                     r=[tmpc, cst], w=[tmpc])
                T.op("dve", lambda e, h=h: e.tensor_tensor(out=DT.ap[:, h, :], in0=tmpc.ap[:, 0, :], in1=tmpc.ap[:, 1, :], op=ALU.add),
                     r=[tmpc], w=[DT])
                for dc in range(2):
                    T.op("act", lambda e, h=h, dc=dc: e.activation(out=qdF.ap[:, 2 * h + dc, :], in_=cNp1(), func=AF.Exp,
                                                                   scale=lg.ap[:, h:h + 1]), r=[cst, lg], w=[qdF])
                    T.op("act", lambda e, h=h, dc=dc: e.activation(out=qdB.ap[:, 2 * h + dc, :], in_=cCmn(), func=AF.Exp,
                                                                   scale=lg.ap[:, 4 + h:5 + h]), r=[cst, lg], w=[qdB])

        def subs(whole, n):
            nb = len(whole.keys) // n
            assert nb * n == len(whole.keys)
            return [Buf(whole.ap[:, j], whole.keys[j * nb:(j + 1) * nb]) for j in range(n)]

        def norm_a(x_ap, xbuf, gtab, xs, ss, rstd, junk):
            T.op("act", lambda e: e.activation(out=xs.ap, in_=x_ap, func=AF.Square, accum_out=ss.ap[:, 0:1]),
                 r=[xbuf], w=[xs, ss])
            rstd_from_ss(ss, rstd, 1, 1.0 / D, NORM_EPS)
            T.op("dve", lambda e: e.scalar_tensor_tensor(out=xs.ap, in0=x_ap, scalar=rstd.ap[:, 0:1], in1=gtab.ap,
                                                         op0=ALU.mult, op1=ALU.mult), r=[xbuf, rstd, gtab], w=[xs])

        def norm_b(xs, hT, c):
            pb = bank()
            pv = pb.ap.bitcast(BF16)
            for k in range(KD):
                T.op("pe", lambda e, k=k: e.transpose(out=pv[:, k * 128:(k + 1) * 128], in_=xs.ap[:, k * 128:(k + 1) * 128],
                                                      identity=ident.ap), r=[xs, ident], w=[pb], inc=(k == KD - 1))
            T.op("act", lambda e: e.activation(out=hT.ap[:, :, c * 128:(c + 1) * 128], in_=pv.rearrange("p (k n) -> p k n", k=KD),
                                               func=AF.Copy), r=[pb], w=[hT])

        def mm_group(pb, out_ap, pairs, r):
            n = len(pairs)
            for i, (l_ap, r_ap) in enumerate(pairs):
                T.op("pe", lambda e, l_ap=l_ap, r_ap=r_ap, i=i: e.matmul(out_ap, lhsT=l_ap, rhs=r_ap, start=(i == 0), stop=(i == n - 1)),
                     r=r, w=[pb], inc=(i == n - 1))

        def sweep1(l, i, xsrc, xsrc_name, mode="full", own=False):
            S = seq_lens[i]
            sc = scr[i]
            A.reset(base_off)
            ring = Ring(6, 3)
            idxb = [A.alloc([1], I32) for _ in range(2)]
            ctab, stab = (cos_own, sin_own) if own else (costab, sintab)
            xt = [A.alloc([D], F32) for _ in range(2)]
            xs = [A.alloc([D], BF16) for _ in range(4)]
            junk = A.alloc([D], BF16)
            ss = [A.alloc([1], F32) for _ in range(4)]
            rstd = [A.alloc([1], F32) for _ in range(4)]
            hTs = [A.alloc([KD, TS], BF16) for _ in range(2)]
            gtab = A.alloc([D], F32)
            gnw = A.alloc([RV], F32)
            gnb = A.alloc([RV], F32)
            ccj = [A.alloc([TS], F32) for _ in range(2)]
            zt = [A.alloc([4, TS], F32) for _ in range(2)]
            cbt = [A.alloc([4, TS], BF16) for _ in range(2)]
            qt = A.alloc([8, TS], BF16)
            kt = A.alloc([8, TS], BF16)
            vt = [A.alloc([4, 512], BF16) for _ in range(2)]
            sgwt = [A.alloc([4, 512], BF16) for _ in range(2)]
            sgbt = [A.alloc([4, 512], BF16) for _ in range(2)]
            gat = [A.alloc([4, TS], BF16) for _ in range(2)]
            rt = [A.alloc([TS], F32) for _ in range(4)]
            sig = [rt[0], rt[1]]
            sgf = [rt[2], rt[3]]
            cs = A.alloc([2, TS], F32)
            T.op("sp", lambda e: e.dma_start(out=gtab.ap, in_=norm_mix[l:l + 1, :].partition_broadcast(128)[:, 0, :]), w=[gtab], dma=True)
            T.op("sp", lambda e: e.dma_start(out=gnw.ap, in_=gn_w[l:l + 1, :].partition_broadcast(128)[:, 0, :]), w=[gnw], dma=True)
            T.op("sp", lambda e: e.dma_start(out=gnb.ap, in_=gn_b[l:l + 1, :].partition_broadcast(128)[:, 0, :]), w=[gnb], dma=True)
            if mode == "full":
                order = [2, 4, 0, 3, 5, 1] + list(range(6, 22))
            elif mode == "kv":
                order = [8, 9, 10, 11, 12, 13]
            else:
                order = [2, 4, 0, 3, 5, 1, 6, 7] + list(range(14, 22))
            pos = {u: k for k, u in enumerate(order)}
            nper = len(order)
            ntile = S // TS
            plan = []
            for t in range(ntile):
                plan += [l * NU_L + u for u in order]
            ring.set_plan(plan)
            cnt = [0]

            def front_a(t):
                for c in range(4):
                    r0 = t * TS + c * C
                    if own:
                        gather_rows(xt[c % 2].ap, xt[c % 2], scr[0]["xb_pad"], oidx_x[r0 // C], idxb[c % 2])
                    else:
                        T.op("sp", lambda e, c=c, r0=r0: e.dma_start(out=xt[c % 2].ap, in_=xsrc[r0:r0 + C, :]),
                             r=[dkey(xsrc_name, i, r0 // C)], w=[xt[c % 2]], dma=True)
                    norm_a(xt[c % 2].ap, xt[c % 2], gtab, xs[c], ss[c], rstd[c], junk)

            def front_b(t):
                for c in range(4):
                    norm_b(xs[c], hTs[t % 2], c)

            front_a(0)
            front_b(0)
            for t in range(ntile):
                t0 = t * TS
                base = t * nper
                hT = hTs[t % 2]
                ring.prefetch()
                T.op("sp", lambda e, t0=t0: e.dma_start(out=cs.ap[:, 0, :], in_=ctab[:, t0:t0 + TS]), w=[cs], dma=True)
                T.op("sp", lambda e, t0=t0: e.dma_start(out=cs.ap[:, 1, :], in_=stab[:, t0:t0 + TS]), w=[cs], dma=True)

                def fm(pos, jj):
                    wslot = ring.get(base + pos)
                    pb = bank()
                    mm_group(pb, pb.ap, [(wslot.ap[:, k, jj * 128:(jj + 1) * 128], hT.ap[:, k, :]) for k in range(KD)], [wslot, hT])
                    return pb

                def tm(pos, c):
                    wslot = ring.get(base + pos)
                    pb = bank()
                    mm_group(pb, pb.ap, [(hT.ap[:, k, c * 128:(c + 1) * 128], wslot.ap[:, k, :]) for k in range(KD)], [wslot, hT])
                    return pb

                for half in (range(2) if mode != "kv" else ()):
                    ztb, cbb = zt[half], cbt[half]
                    for jj in range(4):
                        j = half * 4 + jj
                        pcc = fm(half * 3 + 0, jj)
                        cj = ccj[j % 2]
                        T.op("act", lambda e, pcc=pcc, cj=cj: e.activation(out=cj.ap, in_=pcc.ap, func=AF.Copy), r=[pcc], w=[cj])
                        pcx = fm(half * 3 + 1, jj)
                        T.op("dve", lambda e, pcx=pcx, cj=cj, jj=jj, ztb=ztb: e.tensor_tensor(out=ztb.ap[:, jj, :], in0=pcx.ap, in1=cj.ap, op=ALU.mult),
                             r=[pcx, cj], w=[ztb])
                        pcb = fm(half * 3 + 2, jj)
                        T.op("act", lambda e, pcb=pcb, jj=jj, cbb=cbb: e.activation(out=cbb.ap[:, jj, :], in_=pcb.ap, func=AF.Copy), r=[pcb], w=[cbb])
                    T.op("act", lambda e, t0=t0, half=half, ztb=ztb: e.dma_start(
                        out=sc["zT"].rearrange("j p s -> p j s")[:, half * 4:half * 4 + 4, 1 + t0:1 + t0 + TS], in_=ztb.ap),
                        r=[ztb], w=[dkey("zT", i, t, half)], dma=True)
                    T.op("act", lambda e, t0=t0, half=half, cbb=cbb: e.dma_start(
                        out=sc["cbT"].rearrange("j p s -> p j s")[:, half * 4:half * 4 + 4, t0:t0 + TS], in_=cbb.ap),
                        r=[cbb], w=[dkey("cbT", i, t, half)], dma=True)
                if t + 1 < ntile:
                    front_a(t + 1)
                for which, dst, dname in ((0, qt, "qT"), (1, kt, "kT")):
                    if (which == 0 and mode == "kv") or (which == 1 and mode == "rest"):
                        continue
                    for uu in range(2):
                        for hh in range(2):
                            h = uu * 2 + hh
                            p0 = fm(pos[6 + which * 2 + uu], hh * 2)
                            p1 = fm(pos[6 + which * 2 + uu], hh * 2 + 1)
                            T.op("dve", lambda e, p0=p0: e.tensor_tensor(out=rt[0].ap, in0=p0.ap, in1=cs.ap[:, 0, :], op=ALU.mult), r=[p0, cs], w=[rt[0]])
                            T.op("dve", lambda e, p1=p1: e.tensor_tensor(out=rt[1].ap, in0=p1.ap, in1=cs.ap[:, 1, :], op=ALU.mult), r=[p1, cs], w=[rt[1]])
                            T.op("pool", lambda e, dst=dst, h=h: e.tensor_tensor(out=dst.ap[:, 2 * h, :], in0=rt[0].ap, in1=rt[1].ap, op=ALU.subtract),
                                 r=[rt[0], rt[1]], w=[dst])
                            T.op("dve", lambda e, p0=p0: e.tensor_tensor(out=rt[2].ap, in0=p0.ap, in1=cs.ap[:, 1, :], op=ALU.mult), r=[p0, cs], w=[rt[2]])
                            T.op("dve", lambda e, p1=p1: e.tensor_tensor(out=rt[3].ap, in0=p1.ap, in1=cs.ap[:, 0, :], op=ALU.mult), r=[p1, cs], w=[rt[3]])
                            T.op("pool", lambda e, dst=dst, h=h: e.tensor_tensor(out=dst.ap[:, 2 * h + 1, :], in0=rt[2].ap, in1=rt[3].ap, op=ALU.add),
                                 r=[rt[2], rt[3]], w=[dst])
                    for c in range(4):
                        ci = t * 4 + c
                        T.op("act", lambda e, dst=dst, dname=dname, ci=ci, c=c: e.dma_start(
                            out=sc[dname][ci].rearrange("p (j n) -> p j n", j=8), in_=dst.ap[:, :, c * 128:(c + 1) * 128]),
                            r=[dst], w=[dkey(dname, i, ci)], dma=True)
                if t + 1 < ntile:
                    front_b(t + 1)
                for h in (range(H) if mode != "rest" else ()):
                    vb = vt[h % 2]
                    for c in range(4):
                        pv_ = tm(pos[10 + h], c)
                        if c % 2 == 0:
                            T.op("act", lambda e, pv_=pv_, c=c, vb=vb: e.activation(out=vb.ap[:, c, :], in_=pv_.ap, func=AF.Copy), r=[pv_], w=[vb])
                        else:
                            T.op("dve", lambda e, pv_=pv_, c=c, vb=vb: e.tensor_copy(out=vb.ap[:, c, :], in_=pv_.ap), r=[pv_], w=[vb])
                    T.op("act", lambda e, t0=t0, h=h, vb=vb: e.dma_start(
                        out=sc["v"][t0:t0 + TS, h * 512:(h + 1) * 512].rearrange("(c p) e -> p c e", p=128), in_=vb.ap),
                        r=[vb], w=[dkey("v", i, t, h)], dma=True)
                for h in (range(H) if mode != "kv" else ()):
                    wb, bb = sgwt[h % 2], sgbt[h % 2]
                    for c in range(4):
                        pg = tm(pos[14 + h], c)
                        sb_ = sig[(h * 4 + c) % 2]
                        sf_ = sgf[(h * 4 + c) % 2]
                        T.op("act", lambda e, pg=pg, sb_=sb_: e.activation(out=sb_.ap, in_=pg.ap, func=AF.Sigmoid), r=[pg], w=[sb_])
                        T.op("dve", lambda e, pg=pg, sb_=sb_, sf_=sf_: e.tensor_tensor(out=sf_.ap, in0=pg.ap, in1=sb_.ap, op=ALU.mult),
                             r=[pg, sb_], w=[sf_])
                        T.op("dve", lambda e, sf_=sf_, c=c, h=h, wb=wb: e.tensor_tensor(out=wb.ap[:, c, :], in0=sf_.ap, in1=gnw.ap[:, h * 512:(h + 1) * 512], op=ALU.mult),
                             r=[sf_, gnw], w=[wb])
                        T.op("pool", lambda e, sf_=sf_, c=c, h=h, bb=bb: e.tensor_tensor(out=bb.ap[:, c, :], in0=sf_.ap, in1=gnb.ap[:, h * 512:(h + 1) * 512], op=ALU.mult),
                             r=[sf_, gnb], w=[bb])
                    T.op("act", lambda e, t0=t0, h=h, wb=wb: e.dma_start(
                        out=sc["sgw"][t0:t0 + TS, h * 512:(h + 1) * 512].rearrange("(c p) e -> p c e", p=128), in_=wb.ap),
                        r=[wb], w=[dkey("sgw", i, t, h)], dma=True)
                    T.op("act", lambda e, t0=t0, h=h, bb=bb: e.dma_start(
                        out=sc["sgb"][t0:t0 + TS, h * 512:(h + 1) * 512].rearrange("(c p) e -> p c e", p=128), in_=bb.ap),
                        r=[bb], w=[dkey("sgb", i, t, h)], dma=True)
                for uu in (range(4) if mode != "kv" else ()):
                    gb_ = gat[uu % 2]
                    for jj in range(4):
                        f = uu * 4 + jj
                        pg = fm(pos[18 + uu], jj)
                        T.op("act", lambda e, pg=pg, f=f, jj=jj, gb_=gb_: e.activation(out=gb_.ap[:, jj, :], in_=pg.ap, func=AF.Sigmoid, bias=gbias.ap[:, f:f + 1]),
                             r=[pg, gbias], w=[gb_])
                    T.op("act", lambda e, t0=t0, uu=uu, gb_=gb_: e.dma_start(
                        out=sc["gaT"].rearrange("j p s -> p j s")[:, uu * 4:uu * 4 + 4, t0:t0 + TS], in_=gb_.ap),
                        r=[gb_], w=[dkey("gaT", i, t, uu)], dma=True)

        def kdec_transposes(ktc, Kx, doff, pb):
            pv = pb.ap.bitcast(BF16)
            for j in range(8):
                T.op("pe", lambda e, j=j: e.transpose(out=pv[:, j * 128:(j + 1) * 128], in_=ktc.ap[:, j, :], identity=ident.ap),
                     r=[ktc, ident], w=[pb], inc=(j == 7))
            for h in range(H):
                T.op("act", lambda e, h=h: e.activation(out=Kx.ap[:, h * 256:(h + 1) * 256], in_=pv[:, h * 256:(h + 1) * 256],
                                                        func=AF.Copy, scale=kdec.ap[:, doff + h:doff + h + 1]), r=[pb, kdec], w=[Kx])

        def state_update(Kx, vch, S32in, S32out, S16n, doff, banks):
            for h in range(H):
                for dc in range(2):
                    j = 2 * h + dc
                    pb = banks[j % len(banks)]
                    mm_group(pb, pb.ap, [(Kx.ap[:, j * 128:(j + 1) * 128], vch.ap[:, h * 512:(h + 1) * 512])], [Kx, vch])
                    T.op("dve", lambda e, pb=pb, j=j, h=h: e.scalar_tensor_tensor(
                        out=S32out[j].ap, in0=S32in[j].ap, scalar=cdec.ap[:, doff + h:doff + h + 1], in1=pb.ap,
                        op0=ALU.mult, op1=ALU.add), r=[S32in[j], cdec, pb], w=[S32out[j]])
                    if S16n is not None:
                        T.op("act", lambda e, j=j: e.activation(out=S16n[j].ap, in_=S32out[j].ap, func=AF.Copy), r=[S32out[j]], w=[S16n[j]])

        def sweep2a(l, i, fwd=False):
            S = seq_lens[i]
            sc = scr[i]
            nch = S // C
            doff = 0 if fwd else 4
            sname = "sf" if fwd else "sb"
            chunk_order = list(range(nch)) if fwd else list(range(nch - 1, -1, -1))
            A.reset(base_off)
            ktc = [A.alloc([8, 128], BF16) for _ in range(2)]
            vch = [A.alloc([RV], BF16) for _ in range(2)]
            Kx = [A.alloc([D], BF16) for _ in range(2)]
            S32w = [A.alloc([8, 512], F32) for _ in range(2)]
            S32 = [subs(w_, 8) for w_ in S32w]
            T.op("pool", lambda e: e.memset(S32w[0].ap, 0.0), w=[S32w[0]])
            for n_, ci in enumerate(chunk_order):
                b = n_ % 2
                T.op("sp", lambda e, ci=ci, b=b: e.dma_start(out=ktc[b].ap, in_=sc["kT"][ci].rearrange("p (j n) -> p j n", j=8)),
                     r=[dkey("kT", i, ci)], w=[ktc[b]], dma=True)
                T.op("sp", lambda e, ci=ci, b=b: e.dma_start(out=vch[b].ap, in_=sc["v"][ci * C:(ci + 1) * C, :]),
                     r=[dkey("v", i, ci // 4, h) for h in range(H)], w=[vch[b]], dma=True)
                T.op("pool", lambda e, ci=ci, b=b: e.dma_start(out=sc[sname][ci].rearrange("p (j n) -> p j n", j=8), in_=S32w[b].ap),
                     r=[S32w[b]], w=[dkey(sname, i, ci)], dma=True)
                if n_ + 1 < nch:
                    kdec_transposes(ktc[b], Kx[b], doff, bank())
                    state_update(Kx[b], vch[b], S32[b], S32[1 - b], None, doff, [bank() for _ in range(4)])

        def sweep2b(l, i, own=False):
            S = seq_lens[i]
            sc = scr[i]
            nch = (OWN // C) if own else (S // C)
            lo = 4 if own else 0
            g0 = scr[0]
            A.reset(base_off)
            idxb = [A.alloc([1], I32) for _ in range(2)]
            sfc = [A.alloc([8, 512], BF16) for _ in range(2)]
            qtc = [A.alloc([8, 128], BF16) for _ in range(2)]
            ktc = [A.alloc([8, 128], BF16) for _ in range(2)]
            vch = [A.alloc([RV], BF16) for _ in range(2)]
            sgwc = [A.alloc([4, 512], BF16) for _ in range(2)]
            sgbc = [A.alloc([4, 512], BF16) for _ in range(2)]
            sbc = [A.alloc([8, 512], BF16) for _ in range(2)]
            Kx = [A.alloc([D], BF16) for _ in range(2)]
            S32w = A.alloc([8, 512], F32)
            S16w = [A.alloc([8, 512], BF16) for _ in range(2)]
            S32 = subs(S32w, 8)
            S16 = [subs(w_, 8) for w_ in S16w]
            Qf = [A.alloc([8, 128], BF16) for _ in range(2)]
            Qb = [A.alloc([8, 128], BF16) for _ in range(2)]
            PT = [A.alloc([512], BF16) for _ in range(2)]
            st6 = [A.alloc([H, 6], F32) for _ in range(2)]
            mv = [A.alloc([H, 2], F32) for _ in range(2)]
            ve = [A.alloc([H], F32) for _ in range(2)]
            rs4 = [A.alloc([H], F32) for _ in range(2)]
            nb4 = [A.alloc([H], F32) for _ in range(2)]
            on = subs(A.alloc([H, 512], F32), H)
            t1 = subs(A.alloc([H, 512], F32), H)
            gatedw = [A.alloc([H, 512], BF16) for _ in range(2)]
            gated = [subs(w_, H) for w_ in gatedw]
            gTt = [A.alloc([16, 128], BF16) for _ in range(2)]
            DT = A.alloc([H, 128], F32)
            qdF = A.alloc([8, 128], F32)
            qdB = A.alloc([8, 128], F32)
            tmpc = A.alloc([2, 128], F32)
            ret_consts(l, DT, qdF, qdB, tmpc)
            T.op("pool", lambda e: e.memset(S32w.ap, 0.0), w=[S32w])
            T.op("pool", lambda e: e.memset(S16w[0].ap, 0.0), w=[S16w[0]])
            P = ps_bufs

            def loads_a(ci):
                b = ci % 2
                if own:
                    lk = lo + ci
                    T.op("sp", lambda e: e.dma_start(out=qtc[b].ap, in_=sc["qT"][lk].rearrange("p (j n) -> p j n", j=8)),
                         r=[dkey("qT", i, lk)], w=[qtc[b]], dma=True)
                    T.op("sp", lambda e: e.dma_start(out=idxb[b].ap, in_=oidx_c[lk]), w=[idxb[b]], dma=True)
                    uidx = bass.IndirectOffsetOnAxis(ap=idxb[b].ap.bitcast(U32), axis=0)
                    for dst, src in ((ktc[b], g0["kT"].rearrange("c p f -> (c p) f")), (vch[b], g0["v"]),
                                     (sbc[b], g0["sb"].rearrange("c p f -> (c p) f")), (sfc[b], g0["sf"].rearrange("c p f -> (c p) f"))):
                        flat = dst.ap if len(dst.ap.shape) == 2 else dst.ap.rearrange("p j n -> p (j n)")
                        T.op("pool", lambda e, flat=flat, src=src: e.indirect_dma_start(out=flat, out_offset=None, in_=src, in_offset=uidx),
                             r=[idxb[b]], w=[dst], dma=True)
                    return
                T.op("sp", lambda e: e.dma_start(out=qtc[b].ap, in_=sc["qT"][ci].rearrange("p (j n) -> p j n", j=8)),
                     r=[dkey("qT", i, ci)], w=[qtc[b]], dma=True)
                T.op("sp", lambda e: e.dma_start(out=ktc[b].ap, in_=sc["kT"][ci].rearrange("p (j n) -> p j n", j=8)),
                     r=[dkey("kT", i, ci)], w=[ktc[b]], dma=True)
                T.op("sp", lambda e: e.dma_start(out=vch[b].ap, in_=sc["v"][ci * C:(ci + 1) * C, :]),
                     r=[dkey("v", i, ci // 4, h) for h in range(H)], w=[vch[b]], dma=True)
                T.op("sp", lambda e: e.dma_start(out=sbc[b].ap, in_=sc["sb"][ci].rearrange("p (j n) -> p j n", j=8)),
                     r=[dkey("sb", i, ci)], w=[sbc[b]], dma=True)

            def loads_g(ci):
                b = ci % 2
                lk = lo + ci
                T.op("sp", lambda e: e.dma_start(out=sgwc[b].ap, in_=sc["sgw"][lk * C:(lk + 1) * C, :].rearrange("p (h e) -> p h e", h=H)),
                     r=[dkey("sgw", i, lk // 4, h) for h in range(H)], w=[sgwc[b]], dma=True)
                T.op("sp", lambda e: e.dma_start(out=sgbc[b].ap, in_=sc["sgb"][lk * C:(lk + 1) * C, :].rearrange("p (h e) -> p h e", h=H)),
                     r=[dkey("sgb", i, lk // 4, h) for h in range(H)], w=[sgbc[b]], dma=True)

            def stage_q(ci):
                b = ci % 2
                T.op("pool", lambda e: e.tensor_tensor(out=Qf[b].ap, in0=qtc[b].ap, in1=qdF.ap, op=ALU.mult), r=[qtc[b], qdF], w=[Qf[b]])
                T.op("pool", lambda e: e.tensor_tensor(out=Qb[b].ap, in0=qtc[b].ap, in1=qdB.ap, op=ALU.mult), r=[qtc[b], qdB], w=[Qb[b]])

            def stage_s(ci):
                b = ci % 2
                psS = P[4]
                for h in range(H):
                    mm_group(psS, psS.ap[:, h * 128:(h + 1) * 128],
                             [(ktc[b].ap[:, 2 * h + dc, :], qtc[b].ap[:, 2 * h + dc, :]) for dc in range(2)], [ktc[b], qtc[b]])
                T.op("dve", lambda e: e.tensor_tensor(out=PT[b].ap, in0=psS.ap, in1=DT.ap.rearrange("p h n -> p (h n)"), op=ALU.mult),
                     r=[psS, DT], w=[PT[b]])

            def stage_o(ci, upd):
                b = ci % 2
                for h in range(H):
                    po = P[h]
                    pairs = [(PT[b].ap[:, h * 128:(h + 1) * 128], vch[b].ap[:, h * 512:(h + 1) * 512])]
                    rr = [PT[b], vch[b], Qf[b], Qb[b], sbc[b]]
                    for dc in range(2):
                        if own:
                            pairs.append((Qf[b].ap[:, 2 * h + dc, :], sfc[b].ap[:, 2 * h + dc, :]))
                            rr.append(sfc[b])
                        else:
                            pairs.append((Qf[b].ap[:, 2 * h + dc, :], S16[b][2 * h + dc].ap))
                            rr.append(S16[b][2 * h + dc])
                    for dc in range(2):
                        pairs.append((Qb[b].ap[:, 2 * h + dc, :], sbc[b].ap[:, 2 * h + dc, :]))
                    mm_group(po, po.ap, pairs, rr)
                    T.op("dve", lambda e, po=po, h=h: e.bn_stats(out=st6[b].ap[:, h, :], in_=po.ap), r=[po], w=[st6[b]])
                    T.op("dve", lambda e, h=h: e.bn_aggr(out=mv[b].ap[:, h, :], in_=st6[b].ap[:, h, :]), r=[st6[b]], w=[mv[b]])
                    if upd is not None:
                        state_update_head(upd, h)
                T.op("dve", lambda e: e.tensor_scalar(out=ve[b].ap, in0=mv[b].ap[:, :, 1], scalar1=GN_EPS, scalar2=None, op0=ALU.add), r=[mv[b]], w=[ve[b]])
                T.op("pool", lambda e: e.tensor_tensor(out=rs4[b].ap, in0=ve[b].ap, in1=neghalf.ap[:, 0:4], op=ALU.pow), r=[ve[b], neghalf], w=[rs4[b]])

            def state_update_head(cu, h):
                bu = cu % 2
                for dc in range(2):
                    j = 2 * h + dc
                    pb = P[6 + dc]
                    mm_group(pb, pb.ap, [(Kx[bu].ap[:, j * 128:(j + 1) * 128], vch[bu].ap[:, h * 512:(h + 1) * 512])], [Kx[bu], vch[bu]])
                    T.op("dve", lambda e, pb=pb, j=j, h=h: e.scalar_tensor_tensor(
                        out=S32[j].ap, in0=S32[j].ap, scalar=cdec.ap[:, h:h + 1], in1=pb.ap,
                        op0=ALU.mult, op1=ALU.add), r=[S32[j], cdec, pb], w=[S32[j]])
                    T.op("act", lambda e, j=j: e.activation(out=S16[1 - bu][j].ap, in_=S32[j].ap, func=AF.Copy), r=[S32[j]], w=[S16[1 - bu][j]])

            def stage_k(cu):
                bu = cu % 2
                kdec_transposes(ktc[bu], Kx[bu], 0, P[5])

            def stage_a2(ci):
                b = ci % 2
                T.op("dve", lambda e: e.scalar_tensor_tensor(out=nb4[b].ap, in0=mv[b].ap[:, :, 0], scalar=-1.0, in1=rs4[b].ap, op0=ALU.mult, op1=ALU.mult),
                     r=[mv[b], rs4[b]], w=[nb4[b]])
                for h in range(H):
                    T.op("act", lambda e, h=h: e.activation(out=on[h].ap, in_=P[h].ap, func=AF.Identity,
                                                            scale=rs4[b].ap[:, h:h + 1], bias=nb4[b].ap[:, h:h + 1]),
                         r=[P[h], rs4[b], nb4[b]], w=[on[h]])
                    T.op("dve", lambda e, h=h: e.tensor_tensor(out=t1[h].ap, in0=on[h].ap, in1=sgwc[b].ap[:, h, :], op=ALU.mult),
                         r=[on[h], sgwc[b]], w=[t1[h]])
                    T.op("pool", lambda e, h=h: e.tensor_tensor(out=gated[b][h].ap, in0=t1[h].ap, in1=sgbc[b].ap[:, h, :], op=ALU.add),
                         r=[t1[h], sgbc[b]], w=[gated[b][h]])

            def stage_b(ci):
                b = ci % 2
                for half in range(2):
                    pb = P[5] if half == 0 else P[4]
                    pv = pb.ap.bitcast(BF16)
                    for jj in range(8):
                        f = half * 8 + jj
                        T.op("pe", lambda e, pv=pv, jj=jj, f=f: e.transpose(out=pv[:, jj * 128:(jj + 1) * 128],
                                                                            in_=gated[b][f // 4].ap[:, (f % 4) * 128:(f % 4 + 1) * 128],
                                                                            identity=ident.ap), r=[gated[b][f // 4], ident], w=[pb], inc=(jj == 7))
                    T.op("act", lambda e, pv=pv, half=half: e.activation(out=gTt[b].ap[:, half * 8:half * 8 + 8, :], in_=pv.rearrange("p (j n) -> p j n", j=8), func=AF.Copy),
                         r=[pb], w=[gTt[b]])
                T.op("act", lambda e: e.dma_start(out=sc["gT"][lo + ci].rearrange("p (j n) -> p j n", j=16), in_=gTt[b].ap),
                     r=[gTt[b]], w=[dkey("gT", i, lo + ci)], dma=True)

            loads_a(0)
            loads_g(0)
            if nch > 1:
                loads_a(1)
            stage_q(0)
            for n in range(nch + 1):
                if n < nch:
                    stage_s(n)
                    if n + 1 < nch and not own:
                        stage_k(n)
                if n >= 1:
                    stage_a2(n - 1)
                    if n < nch:
                        loads_g(n)
                if n < nch:
                    stage_o(n, n if (n + 1 < nch and not own) else None)
                    if n + 2 < nch:
                        loads_a(n + 2)
                if n >= 1:
                    stage_b(n - 1)
                if n + 1 < nch:
                    stage_q(n + 1)

        def sweep3a(l, i, xsrc, xsrc_name, xdst, xdst_name, own=False):
            S = seq_lens[i]
            sc = scr[i]
            A.reset(base_off)
            ring = Ring(6, 3)
            idxb = [A.alloc([1], I32) for _ in range(2)]
            zts = [A.alloc([8, TS + 2], F32) for _ in range(2)]
            cbts = [A.alloc([8, TS], BF16) for _ in range(2)]
            uTs = [A.alloc([8, TS], BF16) for _ in range(2)]
            gat = A.alloc([16, TS], BF16)
            gTt = A.alloc([16, TS], BF16)
            xt = A.alloc([4, D], F32)
            ca = [A.alloc([TS], F32) for _ in range(2)]
            cb_ = [A.alloc([TS], F32) for _ in range(2)]
            ta = [A.alloc([TS], F32) for _ in range(2)]
            tb = [A.alloc([TS], F32) for _ in range(2)]
            mT = A.alloc([8, TS], BF16)
            ntile = S // TS
            per = [22, 24, 26, 23, 25, 27, 28, 29]
            tl = list(range(1, OT + 1)) if own else list(range(ntile))
            plan = []
            for t in tl:
                plan += [l * NU_L + u for u in per]
            ring.set_plan(plan)

            def loads_conv(t):
                t0 = t * TS
                zt, cbt = zts[t % 2], cbts[t % 2]
                zkeys = [dkey("zT", i, t, 0), dkey("zT", i, t, 1)]
                if t > 0:
                    zkeys += [dkey("zT", i, t - 1, 0), dkey("zT", i, t - 1, 1)]
                else:
                    zkeys.append(dkey("zTh", i, 0))
                if t + 1 < ntile:
                    zkeys += [dkey("zT", i, t + 1, 0), dkey("zT", i, t + 1, 1)]
                else:
                    zkeys.append(dkey("zTh", i, S + 1))
                T.op("sp", lambda e: e.dma_start(out=zt.ap, in_=sc["zT"].rearrange("j p s -> p j s")[:, :, t0:t0 + TS + 2]),
                     r=zkeys, w=[zt], dma=True)
                T.op("sp", lambda e: e.dma_start(out=cbt.ap, in_=sc["cbT"].rearrange("j p s -> p j s")[:, :, t0:t0 + TS]),
                     r=[dkey("cbT", i, t, 0), dkey("cbT", i, t, 1)], w=[cbt], dma=True)

            def loads_g(t):
                t0 = t * TS
                T.op("sp", lambda e: e.dma_start(out=gat.ap, in_=sc["gaT"].rearrange("j p s -> p j s")[:, :, t0:t0 + TS]),
                     r=[dkey("gaT", i, t, uu) for uu in range(4)], w=[gat], dma=True)
                for c in range(4):
                    ci = t * 4 + c
                    T.op("sp", lambda e, ci=ci, c=c: e.dma_start(out=gTt.ap[:, :, c * 128:(c + 1) * 128],
                                                                 in_=sc["gT"][ci].rearrange("p (j n) -> p j n", j=16)),
                         r=[dkey("gT", i, ci)], w=[gTt], dma=True)

            def loads_x(t):
                t0 = t * TS
                if own:
                    for c in range(4):
                        gather_rows(xt.ap[:, c, :], xt, scr[0]["xb_pad"], oidx_x[t * 4 + c], idxb[c % 2])
                    return
                T.op("sp", lambda e: e.dma_start(out=xt.ap, in_=xsrc[t0:t0 + TS, :].rearrange("(c p) d -> p c d", p=128)),
                     r=[dkey(xsrc_name, i, t * 4 + c) for c in range(4)], w=[xt], dma=True)

            def conv(t):
                zt, cbt, uT = zts[t % 2], cbts[t % 2], uTs[t % 2]
                for j in range(8):
                    a_, b_ = ca[j % 2], cb_[j % 2]
                    T.op("act", lambda e, j=j, a_=a_: e.activation(out=a_.ap, in_=zt.ap[:, j, 1:TS + 1], func=AF.Copy, scale=cw.ap[:, 1, j:j + 1]),
                         r=[zt, cw], w=[a_])
                    T.op("dve", lambda e, j=j, a_=a_, b_=b_: e.scalar_tensor_tensor(out=b_.ap, in0=zt.ap[:, j, 0:TS], scalar=cw.ap[:, 0, j:j + 1], in1=a_.ap,
                                                                                  op0=ALU.mult, op1=ALU.add), r=[zt, cw, a_], w=[b_])
                    T.op("dve", lambda e, j=j, a_=a_, b_=b_: e.scalar_tensor_tensor(out=a_.ap, in0=zt.ap[:, j, 2:TS + 2], scalar=cw.ap[:, 2, j:j + 1], in1=b_.ap,
                                                                                  op0=ALU.mult, op1=ALU.add), r=[zt, cw, b_], w=[a_])
                    T.op("pool", lambda e, j=j, a_=a_: e.tensor_tensor(out=uT.ap[:, j, :], in0=a_.ap, in1=cbt.ap[:, j, :], op=ALU.mult),
                         r=[a_, cbt], w=[uT])

            loads_conv(tl[0])
            loads_g(tl[0])
            loads_x(tl[0])
            conv(tl[0])
            for k_, t in enumerate(tl):
                nxt = tl[k_ + 1] if k_ + 1 < len(tl) else None
                t0 = t * TS
                base = k_ * 8
                uT = uTs[t % 2]
                ring.prefetch()
                if nxt is not None:
                    loads_conv(nxt)
                for j in range(8):
                    ch, jj = j // 4, j % 4
                    wco = ring.get(base + ch * 3 + 0)
                    pb = bank()
                    mm_group(pb, pb.ap, [(wco.ap[:, k, jj * 128:(jj + 1) * 128], uT.ap[:, k, :]) for k in range(KD)], [wco, uT])
                    ta_, tb_ = ta[j % 2], tb[j % 2]
                    T.op("dve", lambda e, pb=pb, j=j, ta_=ta_: e.tensor_tensor(out=ta_.ap, in0=pb.ap, in1=gat.ap[:, j, :], op=ALU.mult), r=[pb, gat], w=[ta_])
                    wr0 = ring.get(base + ch * 3 + 1)
                    wr1 = ring.get(base + ch * 3 + 2)
                    pb2 = bank()
                    pairs = [((wr0 if k < 8 else wr1).ap[:, k % 8, jj * 128:(jj + 1) * 128], gTt.ap[:, k, :]) for k in range(16)]
                    mm_group(pb2, pb2.ap, pairs, [wr0, wr1, gTt])
                    T.op("dve", lambda e, pb2=pb2, j=j, tb_=tb_: e.tensor_tensor(out=tb_.ap, in0=pb2.ap, in1=gat.ap[:, 8 + j, :], op=ALU.mult), r=[pb2, gat], w=[tb_])
                    T.op("pool", lambda e, j=j, ta_=ta_, tb_=tb_: e.tensor_tensor(out=mT.ap[:, j, :], in0=ta_.ap, in1=tb_.ap, op=ALU.add), r=[ta_, tb_], w=[mT])
                if nxt is not None:
                    loads_g(nxt)
                    conv(nxt)
                for ch in range(2):
                    wm = ring.get(base + 6 + ch)
                    for c in range(4):
                        pb = bank()
                        mm_group(pb, pb.ap, [(mT.ap[:, k, c * 128:(c + 1) * 128], wm.ap[:, k, :]) for k in range(KD)], [wm, mT])
                        T.op("dve", lambda e, pb=pb, c=c, ch=ch: e.tensor_tensor(out=xt.ap[:, c, ch * 512:(ch + 1) * 512], in0=pb.ap,
                                                                                 in1=xt.ap[:, c, ch * 512:(ch + 1) * 512], op=ALU.add), r=[pb, xt], w=[xt])
                T.op("act", lambda e, t0=t0: e.dma_start(out=xdst[t0:t0 + TS, :].rearrange("(c p) d -> p c d", p=128), in_=xt.ap),
                     r=[xt], w=[dkey(xdst_name, i, t * 4 + c) for c in range(4)], dma=True)
                if nxt is not None:
                    loads_x(nxt)

        def sweep3b(l, i, xsrc, xsrc_name, xdst, xdst_name, final, own=False):
            S = seq_lens[i]
            A.reset(base_off)
            ring = Ring(6, 1)
            xt = [A.alloc([4, D], F32) for _ in range(2)]
            xs = [A.alloc([D], BF16) for _ in range(4)]
            junk = A.alloc([D], BF16)
            junkf = A.alloc([D], F32)
            ss = [A.alloc([1], F32) for _ in range(6)]
            rstd = [A.alloc([1], F32) for _ in range(6)]
            h2Ts = [A.alloc([KD, TS], BF16) for _ in range(2)]
            gtab = A.alloc([D], F32)
            rl = [A.alloc([TS], F32) for _ in range(4)]
            hid = A.alloc([32, TS], BF16)
            nfin = A.alloc([D], F32)
            T.op("sp", lambda e: e.dma_start(out=gtab.ap, in_=norm_mlp[l:l + 1, :].partition_broadcast(128)[:, 0, :]), w=[gtab], dma=True)
            if final:
                T.op("sp", lambda e: e.dma_start(out=nfin.ap, in_=norm_final[0:1, :].partition_broadcast(128)[:, 0, :]), w=[nfin], dma=True)
            ntile = S // TS
            per = list(range(30, 38)) + [38, 40, 42, 44, 39, 41, 43, 45]
            tl = list(range(1, OT + 1)) if own else list(range(ntile))
            dst_off = -TS if own else 0
            plan = []
            for t in tl:
                plan += [l * NU_L + u for u in per]
            ring.set_plan(plan)
            cnt = [0]

            def load(t):
                t0 = t * TS
                T.op("sp", lambda e: e.dma_start(out=xt[t % 2].ap, in_=xsrc[t0:t0 + TS, :].rearrange("(c p) d -> p c d", p=128)),
                     r=[dkey(xsrc_name, i, t * 4 + c) for c in range(4)], w=[xt[t % 2]], dma=True)

            def front_a(t):
                X = xt[t % 2]
                for c in range(4):
                    norm_a(X.ap[:, c, :], X, gtab, xs[c], ss[c], rstd[c], junk)

            def front_b(t):
                for c in range(4):
                    norm_b(xs[c], h2Ts[t % 2], c)

            load(tl[0])
            front_a(tl[0])
            front_b(tl[0])
            for k_, t in enumerate(tl):
                nxt = tl[k_ + 1] if k_ + 1 < len(tl) else None
                t0 = t * TS
                base = k_ * 16
                X = xt[t % 2]
                h2T = h2Ts[t % 2]
                ring.prefetch()
                for u in range(8):
                    wslot = ring.get(base + u)
                    for jj in range(4):
                        f = u * 4 + jj
                        pb = bank()
                        mm_group(pb, pb.ap, [(wslot.ap[:, k, jj * 128:(jj + 1) * 128], h2T.ap[:, k, :]) for k in range(KD)], [wslot, h2T])
                        r_ = rl[f % 4]
                        T.op("act", lambda e, pb=pb, r_=r_: e.activation(out=r_.ap, in_=pb.ap, func=AF.Relu), r=[pb], w=[r_])
                        T.op("dve", lambda e, r_=r_, f=f: e.tensor_tensor(out=hid.ap[:, f, :], in0=r_.ap, in1=r_.ap, op=ALU.mult),
                             r=[r_], w=[hid])
                if nxt is not None:
                    load(nxt)
                    front_a(nxt)
                for ch in range(2):
                    pbs = [bank() for _ in range(4)]
                    for kq in range(4):
                        wslot = ring.get(base + 8 + ch * 4 + kq)
                        for c in range(4):
                            for k in range(8):
                                first = (kq == 0 and k == 0)
                                last = (kq == 3 and k == 7)
                                T.op("pe", lambda e, c=c, k=k, kq=kq, wslot=wslot, first=first, last=last, pbs=pbs: e.matmul(
                                    pbs[c].ap, lhsT=hid.ap[:, kq * 8 + k, c * 128:(c + 1) * 128], rhs=wslot.ap[:, k, :], start=first, stop=last),
                                    r=[hid, wslot], w=[pbs[c]], inc=(k == 7))
                    if ch == 0 and nxt is not None:
                        front_b(nxt)
                    for c in range(4):
                        T.op("dve", lambda e, c=c, ch=ch, pbs=pbs, X=X: e.tensor_tensor(out=X.ap[:, c, ch * 512:(ch + 1) * 512], in0=pbs[c].ap,
                                                                                        in1=X.ap[:, c, ch * 512:(ch + 1) * 512], op=ALU.add), r=[pbs[c], X], w=[X])
                if final:
                    for c in range(4):
                        b = 4 + c % 2
                        T.op("act", lambda e, c=c, b=b, X=X: e.activation(out=xs[c].ap, in_=X.ap[:, c, :], func=AF.Square, accum_out=ss[b].ap[:, 0:1]),
                             r=[X], w=[xs[c], ss[b]])
                        rstd_from_ss(ss[b], rstd[b], 1, 1.0 / D, NORM_EPS)
                        T.op("dve", lambda e, c=c, b=b, X=X: e.scalar_tensor_tensor(out=X.ap[:, c, :], in0=X.ap[:, c, :], scalar=rstd[b].ap[:, 0:1], in1=nfin.ap,
                                                                                  op0=ALU.mult, op1=ALU.mult), r=[X, rstd[b], nfin], w=[X])
                T.op("act", lambda e, t0=t0, X=X: e.dma_start(out=xdst[t0 + dst_off:t0 + dst_off + TS, :].rearrange("(c p) d -> p c d", p=128), in_=X.ap),
                     r=[X], w=[dkey(xdst_name, i, t * 4 + c) for c in range(4)], dma=True)
        import os as _os
        _stop = int(_os.environ.get("MK_STOP", "99"))
        for l in range(depth):
            if _stop < 1:
                break
            layer_consts(l)
            for i in range(nseq):
                sc = scr[i]
                if l == 0:
                    xsrc, xname = xin[i], "xin"
                else:
                    xsrc, xname = sc["xb"], "xb%d" % (l - 1)
                if rlite and i == 0 and l == depth - 1:
                    sweep1(l, 0, xsrc, xname, mode="kv")
                    sweep2a(l, 0)
                    sweep2a(l, 0, fwd=True)
                    dma_barrier("pool")
                    o = nseq
                    sweep1(l, o, None, None, mode="rest", own=True)
                    sweep2b(l, o, own=True)
                    sweep3a(l, o, None, None, scr[o]["xa"], "xa_own", own=True)
                    sweep3b(l, o, scr[o]["xa"], "xa_own", yout[0], "yout", True, own=True)
                    continue
                if _stop >= 2:
                    sweep1(l, i, xsrc, xname)
                if _stop >= 3:
                    sweep2a(l, i)
                if _stop >= 4:
                    sweep2b(l, i)
                if _stop >= 5:
                    sweep3a(l, i, xsrc, xname, sc["xa"], "xa%d" % l)
                final = (l == depth - 1)
                if _stop < 6:
                    continue
                if final:
                    sweep3b(l, i, sc["xa"], "xa%d" % l, yout[i], "yout", True)
                else:
                    sweep3b(l, i, sc["xa"], "xa%d" % l, sc["xb"], "xb%d" % l, False)
        T.finish()

        with nc.Block() as block:
            @block.tensor
            def _(e):
                T.replay("pe", e)

            @block.scalar
            def _(e):
                T.replay("act", e)

            @block.vector
            def _(e):
                T.replay("dve", e)

            @block.gpsimd
            def _(e):
                T.replay("pool", e)

            @block.sync
            def _(e):
                T.replay("sp", e)
    return nc


def _tables(smax):
    pos = np.arange(smax, dtype=np.float32)
    theta = (1.0 / (10000.0 ** np.linspace(0.0, 1.0, DK // 2, dtype=np.float32))).astype(np.float32)
    ang = (pos[None, :] * theta[:, None]).astype(np.float32)
    cos = np.cos(ang.astype(np.float64)).astype(np.float32)
    sin = np.sin(ang.astype(np.float64)).astype(np.float32)
    n = np.arange(128, dtype=np.float32)
    diff = n[None, :] - n[:, None]
    cst = np.zeros((128, 6 * 128 + 2), np.float32)
    cst[:, 0:128] = np.maximum(diff, 0)
    cst[:, 128:256] = np.maximum(-diff, 0)
    cst[:, 256:384] = (diff >= 0) * 0.0625
    cst[:, 384:512] = (diff < 0) * 0.0625
    cst[:, 512:640] = n[None, :] + 1.0
    cst[:, 640:768] = 128.0 - n[None, :]
    cst[:, 768] = n
    cst[:, 769] = 127.0 - n
    return np.ascontiguousarray(cos), np.ascontiguousarray(sin), cst


_CACHE = {}


def _own_inputs(rank, s0, cos, sin):
    own = s0 // 8
    ot = own // TS
    sl = (ot + 2) * TS
    nlc = sl // C
    p = np.arange(128, dtype=np.int64)
    g = rank * (own // C) - 4 + np.arange(nlc, dtype=np.int64)
    oidx_x = (TS + g[:, None] * C + p[None, :]).astype(np.int32)[:, :, None]
    gc = np.clip(g, 0, s0 // C - 1)
    oidx_c = (gc[:, None] * C + p[None, :]).astype(np.int32)[:, :, None]
    posn = rank * own - TS + np.arange(sl, dtype=np.int64)
    ok = (posn >= 0) & (posn < s0)
    pc = np.clip(posn, 0, s0 - 1)
    cos_own = np.where(ok[None, :], cos[:, pc], 0.0).astype(np.float32)
    sin_own = np.where(ok[None, :], sin[:, pc], 0.0).astype(np.float32)
    return dict(oidx_x=np.ascontiguousarray(oidx_x), oidx_c=np.ascontiguousarray(oidx_c),
                cos_own=np.ascontiguousarray(cos_own), sin_own=np.ascontiguousarray(sin_own))


def run(seq_arrays_per_core, params, depth, ranks=None):
    seq_lens = tuple(a.shape[0] for a in seq_arrays_per_core[0])
    smax = max(seq_lens)
    rlite = depth >= 2 and seq_lens[0] % (8 * TS) == 0
    if ranks is None:
        ranks = list(range(len(seq_arrays_per_core)))
    key = (seq_lens, depth)
    if key not in _CACHE:
        _CACHE[key] = build_program(list(seq_lens), depth, smax)
    nc = _CACHE[key]
    cos, sin, cst = _tables(smax)
    shared = {k: np.ascontiguousarray(np.asarray(v, dtype=np.float32)) for k, v in params.items()}
    shared["norm_final"] = shared["norm_final"].reshape(1, D)
    shared["costab"] = cos
    shared["sintab"] = sin
    shared["consts"] = cst
    in_maps = []
    for ci_, seqs in enumerate(seq_arrays_per_core):
        m = dict(shared)
        for i, a in enumerate(seqs):
            m["x%d" % i] = np.ascontiguousarray(a, dtype=np.float32)
        if rlite:
            m.update(_own_inputs(ranks[ci_], seq_lens[0], cos, sin))
        in_maps.append(m)
    ncores = len(seq_arrays_per_core)
    import os as _os
    if _os.environ.get("MK_TRACE"):
        res = run_bass_kernel_spmd(nc, in_maps, core_ids=list(range(ncores)), trace=True)
        print("EXEC_TIME_NS", res.exec_time_ns, flush=True)
    else:
        res = run_bass_kernel_spmd(nc, in_maps, core_ids=list(range(ncores)))
    return [[r["y%d" % i] for i in range(len(seq_lens))] for r in res.results]


def kernel(x_prompt, x_sample, norm_mix, w_in, conv_w, w_conv_out, ret_decay_fwd, ret_decay_bwd,
           ret_gn_w, ret_gn_b, w_ret_out, gate_b, w_mix_out, norm_mlp, w_mlp_in, w_mlp_out, norm_final):
    x_prompt = np.asarray(x_prompt, dtype=np.float32)
    x_sample = np.asarray(x_sample, dtype=np.float32)
    params = dict(norm_mix=norm_mix, w_in=w_in, conv_w=conv_w, w_conv_out=w_conv_out, ret_decay_fwd=ret_decay_fwd,
                  ret_decay_bwd=ret_decay_bwd, ret_gn_w=ret_gn_w, ret_gn_b=ret_gn_b, w_ret_out=w_ret_out, gate_b=gate_b,
                  w_mix_out=w_mix_out, norm_mlp=norm_mlp, w_mlp_in=w_mlp_in, w_mlp_out=w_mlp_out, norm_final=norm_final)
    depth = np.asarray(w_in).shape[0]
    ncores = 8
    per_core = [[x_prompt[0], x_sample[c]] for c in range(ncores)]
    outs = run(per_core, params, depth)
    sp = x_prompt.shape[1]
    piece = sp // ncores
    if outs[0][0].shape[0] == piece:
        y_prompt = np.concatenate([outs[c][0] for c in range(ncores)], axis=0)[None]
    else:
        y_prompt = np.concatenate([outs[c][0][c * piece:(c + 1) * piece] for c in range(ncores)], axis=0)[None]
    y_sample = np.stack([outs[c][1] for c in range(ncores)], axis=0)
    return (y_prompt.astype(np.float32), y_sample.astype(np.float32))
```
